# Optimizing a Trainium2 kernel written in Bass

```python
import jax, jax.numpy as jnp
from jax import lax
import numpy as np

D_MODEL = 1024
BATCH = 8
SEQ = 2048
DEPTH = 4
DEC_BATCH = 16
DEC_SEQ = 16
PAST_LEN = 4096

CHUNK = 64
N_EVEN = (DEPTH + 1) // 2
N_ODD = DEPTH // 2
MIX_WIDTH = D_MODEL
EPS = 1e-6
A_WIDTH = MIX_WIDTH // 2
A_GROUPS = 4
A_HEAD_DIM = A_WIDTH // A_GROUPS
A_CHUNK = 128
B_WIDTH = MIX_WIDTH - A_WIDTH
CONV_WIDTH = 3
C_WIDTH = MIX_WIDTH // 2
POOL_WINDOWS = (2, 4, 8, 16)
C_GROUPS = len(POOL_WINDOWS)
C_GROUP_DIM = C_WIDTH // C_GROUPS
POOL_HIST = max(POOL_WINDOWS) - 1
D_WIDTH = MIX_WIDTH - C_WIDTH
D_HEADS = 8
D_HEAD_DIM = D_WIDTH // D_HEADS
LEFT_CHUNKS = 8
BAND_ROWS = LEFT_CHUNKS * CHUNK
MAX_REL = 64
KV_ROWS = min(BAND_ROWS, PAST_LEN)
KV_PROMPT_ROWS = min(BAND_ROWS, SEQ)
FFN_HIDDEN = (((8 * D_MODEL + 2) // 3 + 255) // 256) * 256
EV_IN = 2 * A_WIDTH + 3 * B_WIDTH
OD_IN = C_WIDTH + 3 * D_WIDTH
NEG_INF = -1e30

kernel_name = "hybrid_streaming_encoder_step"


def rms_norm(x, g):
    xf = x.astype(jnp.float32)
    y = xf * lax.rsqrt(jnp.mean(xf * xf, axis=-1, keepdims=True) + EPS)
    return (y * g.astype(jnp.float32)).astype(x.dtype)


def swiglu(x, wg, wu, wd):
    return (jax.nn.silu(x @ wg) * (x @ wu)) @ wd


def chunk_mlp(u, v, v_gain, w_s, b_s):
    bn, L, _ = v.shape
    vf = v.astype(jnp.float32)
    mu = jnp.mean(vf, axis=-1, keepdims=True)
    var = jnp.mean(jnp.square(vf - mu), axis=-1, keepdims=True)
    vn = ((vf - mu) * lax.rsqrt(var + EPS) * v_gain.astype(jnp.float32)).astype(v.dtype)
    n = min(L, A_CHUNK)
    mask = jnp.tril(jnp.ones((n, n), dtype=bool))
    w = jnp.where(mask[None], w_s[:, :n, :n], 0)
    vc = vn.reshape(bn, L // n, n, A_GROUPS, A_HEAD_DIM)
    s = jnp.einsum('gij,bcjgd->bcigd', w, vc) + b_s[:, :n].T[None, None, :, :, None]
    return u * s.reshape(bn, L, A_WIDTH), vn


def short_conv(z, hist, w):
    L = z.shape[1]
    zp = jnp.concatenate([hist, z], axis=1)
    y = w[0] * zp[:, 0:L]
    for k in range(1, CONV_WIDTH):
        y = y + w[k] * zp[:, k:k + L]
    return y, zp[:, -(CONV_WIDTH - 1):]


def multi_scale_pool(p, hist, pos, w_g, scale):
    bn, L, _ = p.shape
    pp = jnp.concatenate([hist, p], axis=1)
    ppf = pp.astype(jnp.float32)
    cs = jnp.concatenate([jnp.zeros((bn, 1, C_WIDTH), jnp.float32), jnp.cumsum(ppf, axis=1)], axis=1)
    outs = []
    for g, win in enumerate(POOL_WINDOWS):
        lo, hi = g * C_GROUP_DIM, (g + 1) * C_GROUP_DIM
        top = cs[:, POOL_HIST + 1:POOL_HIST + 1 + L, lo:hi]
        bot = cs[:, POOL_HIST + 1 - win:POOL_HIST + 1 - win + L, lo:hi]
        cnt = jnp.minimum(pos + 1, win).astype(jnp.float32)[None, :, None]
        outs.append((top - bot) / cnt - ppf[:, POOL_HIST:, lo:hi])
    pooled = jnp.stack(outs, axis=2).astype(p.dtype)
    y = jnp.einsum('blgc,gcd->blgd', pooled, w_g).reshape(bn, L, C_WIDTH) * scale
    return y, pp[:, -POOL_HIST:]


def rel_bias_lookup(rel_bias, rel):
    return rel_bias[:, jnp.clip(rel, -MAX_REL, MAX_REL) + MAX_REL]


def band_attention_prompt(q, k, v, rel_bias):
    bn, S = q.shape[:2]
    nc = S // CHUNK
    band = (LEFT_CHUNKS + 1) * CHUNK
    chunks = lambda t: t.reshape(bn, nc, CHUNK, D_HEADS, D_HEAD_DIM)
    pad = lambda t: jnp.pad(chunks(t), ((0, 0), (LEFT_CHUNKS, 0), (0, 0), (0, 0), (0, 0)))
    kp, vp = pad(k), pad(v)
    kb = jnp.concatenate([kp[:, j:j + nc] for j in range(LEFT_CHUNKS + 1)], axis=2)
    vb = jnp.concatenate([vp[:, j:j + nc] for j in range(LEFT_CHUNKS + 1)], axis=2)
    qpos = jnp.arange(S, dtype=jnp.int32).reshape(nc, CHUNK)
    kpos = (jnp.arange(nc, dtype=jnp.int32)[:, None] - LEFT_CHUNKS) * CHUNK + jnp.arange(band, dtype=jnp.int32)[None, :]
    bias = rel_bias_lookup(rel_bias, qpos[:, :, None] - kpos[:, None, :])
    scores = jnp.einsum('bcqhd,bckhd->bchqk', chunks(q), kb).astype(jnp.float32) * (D_HEAD_DIM ** -0.5)
    scores = scores + bias.transpose(1, 0, 2, 3)[None].astype(jnp.float32)
    scores = jnp.where((kpos >= 0)[None, :, None, None, :], scores, NEG_INF)
    probs = jax.nn.softmax(scores, axis=-1).astype(v.dtype)
    out = jnp.einsum('bchqk,bckhd->bcqhd', probs, vb)
    return out.reshape(bn, S, D_WIDTH)


def band_attention_sample(q, k, v, k_cache, v_cache, rel_bias):
    bn, L = q.shape[:2]
    W = k_cache.shape[1]
    kk = jnp.concatenate([k_cache, k], axis=1)
    vv = jnp.concatenate([v_cache, v], axis=1)
    qpos = PAST_LEN + jnp.arange(L, dtype=jnp.int32)
    kpos = PAST_LEN - W + jnp.arange(W + L, dtype=jnp.int32)
    bias = rel_bias_lookup(rel_bias, qpos[:, None] - kpos[None, :])
    scores = jnp.einsum('bqhd,bkhd->bhqk', q, kk).astype(jnp.float32) * (D_HEAD_DIM ** -0.5)
    scores = scores + bias[None].astype(jnp.float32)
    probs = jax.nn.softmax(scores, axis=-1).astype(v.dtype)
    return jnp.einsum('bhqk,bkhd->bqhd', probs, vv).reshape(bn, L, D_WIDTH)


def even_mixer(h, conv_hist, w_in, w_out, v_gain, w_s, b_s, conv_w):
    proj = h @ w_in
    u, v, bg, cg, xb = jnp.split(proj, [A_WIDTH, 2 * A_WIDTH, 2 * A_WIDTH + B_WIDTH, 2 * A_WIDTH + 2 * B_WIDTH], axis=-1)
    ya, vn = chunk_mlp(u, v, v_gain, w_s, b_s)
    zc, new_hist = short_conv(cg * xb, conv_hist, conv_w)
    y = jnp.concatenate([ya, bg * zc], axis=-1) @ w_out
    return y, vn, new_hist


def odd_mixer(h, pool_hist, pos, w_in, w_out, c_w, c_scale, rel_bias, k_cache, v_cache):
    bn, L, _ = h.shape
    proj = h @ w_in
    p, q, k, v = jnp.split(proj, [C_WIDTH, C_WIDTH + D_WIDTH, C_WIDTH + 2 * D_WIDTH], axis=-1)
    yc, new_pool = multi_scale_pool(p, pool_hist, pos, c_w, c_scale)
    q = q.reshape(bn, L, D_HEADS, D_HEAD_DIM)
    k = k.reshape(bn, L, D_HEADS, D_HEAD_DIM)
    v = v.reshape(bn, L, D_HEADS, D_HEAD_DIM)
    if k_cache is None:
        yd = band_attention_prompt(q, k, v, rel_bias)
        k_new, v_new = k[:, -KV_PROMPT_ROWS:], v[:, -KV_PROMPT_ROWS:]
    else:
        yd = band_attention_sample(q, k, v, k_cache, v_cache, rel_bias)
        k_new, v_new = k, v
    y = jnp.concatenate([yc, yd], axis=-1) @ w_out
    return y, new_pool, k_new, v_new


def setup_inputs(seed: int = 0) -> dict:
    key = jax.random.key(seed)
    ks = jax.random.split(key, 24)
    f32 = jnp.float32
    nrm = lambda k, shape, s: jax.random.normal(k, shape, f32) * s
    gain = lambda k, shape: 1.0 + 0.05 * jax.random.normal(k, shape, f32)
    return {
        "x_prompt": nrm(ks[0], (BATCH, SEQ, D_MODEL), 1.0),
        "x_sample": nrm(ks[1], (DEC_BATCH, DEC_SEQ, D_MODEL), 1.0),
        "cache_conv": nrm(ks[2], (N_EVEN, DEC_BATCH, CONV_WIDTH - 1, B_WIDTH), 1.0),
        "cache_pool": nrm(ks[3], (N_ODD, DEC_BATCH, POOL_HIST, C_WIDTH), 1.0),
        "cache_k": nrm(ks[4], (N_ODD, DEC_BATCH, KV_ROWS, D_HEADS, D_HEAD_DIM), 1.0),
        "cache_v": nrm(ks[5], (N_ODD, DEC_BATCH, KV_ROWS, D_HEADS, D_HEAD_DIM), 1.0),
        "norm_mix_pre": gain(ks[6], (DEPTH, D_MODEL)),
        "norm_mix_post": gain(ks[7], (DEPTH, D_MODEL)),
        "norm_ffn_pre": gain(ks[8], (DEPTH, D_MODEL)),
        "norm_ffn_post": gain(ks[9], (DEPTH, D_MODEL)),
        "ffn_w_gate": nrm(ks[10], (DEPTH, D_MODEL, FFN_HIDDEN), D_MODEL ** -0.5),
        "ffn_w_up": nrm(ks[11], (DEPTH, D_MODEL, FFN_HIDDEN), D_MODEL ** -0.5),
        "ffn_w_down": nrm(ks[12], (DEPTH, FFN_HIDDEN, D_MODEL), FFN_HIDDEN ** -0.5),
        "ev_w_in": nrm(ks[13], (N_EVEN, D_MODEL, EV_IN), D_MODEL ** -0.5),
        "ev_w_out": nrm(ks[14], (N_EVEN, MIX_WIDTH, D_MODEL), MIX_WIDTH ** -0.5),
        "a_v_gain": gain(ks[15], (N_EVEN, A_WIDTH)),
        "a_spatial_w": nrm(ks[16], (N_EVEN, A_GROUPS, A_CHUNK, A_CHUNK), A_CHUNK ** -0.5),
        "a_spatial_b": gain(ks[17], (N_EVEN, A_GROUPS, A_CHUNK)),
        "b_conv_w": nrm(ks[18], (N_EVEN, CONV_WIDTH, B_WIDTH), CONV_WIDTH ** -0.5),
        "od_w_in": nrm(ks[19], (N_ODD, D_MODEL, OD_IN), D_MODEL ** -0.5),
        "od_w_out": nrm(ks[20], (N_ODD, MIX_WIDTH, D_MODEL), MIX_WIDTH ** -0.5),
        "c_group_w": nrm(ks[21], (N_ODD, C_GROUPS, C_GROUP_DIM, C_GROUP_DIM), C_GROUP_DIM ** -0.5),
        "c_scale": gain(ks[22], (N_ODD, C_WIDTH)),
        "d_rel_bias": nrm(ks[23], (N_ODD, D_HEADS, 2 * MAX_REL + 1), 0.5),
    }


def reference(x_prompt, x_sample, cache_conv, cache_pool, cache_k, cache_v,
              norm_mix_pre, norm_mix_post, norm_ffn_pre, norm_ffn_post,
              ffn_w_gate, ffn_w_up, ffn_w_down,
              ev_w_in, ev_w_out, a_v_gain, a_spatial_w, a_spatial_b, b_conv_w,
              od_w_in, od_w_out, c_group_w, c_scale, d_rel_bias):
    xp, xs = x_prompt, x_sample
    pos_p = jnp.arange(SEQ, dtype=jnp.int32)
    pos_s = PAST_LEN + jnp.arange(DEC_SEQ, dtype=jnp.int32)
    conv_p, conv_s, av_s = [], [], []
    pool_p, pool_s, kp_l, vp_l, ks_l, vs_l = [], [], [], [], [], []
    for layer in range(DEPTH):
        hp = rms_norm(xp, norm_mix_pre[layer])
        hs = rms_norm(xs, norm_mix_pre[layer])
        if layer % 2 == 0:
            e = layer // 2
            prm = (ev_w_in[e], ev_w_out[e], a_v_gain[e], a_spatial_w[e], a_spatial_b[e], b_conv_w[e])
            zero_hist = jnp.zeros((xp.shape[0], CONV_WIDTH - 1, B_WIDTH), xp.dtype)
            mp, _, hist_p = even_mixer(hp, zero_hist, *prm)
            ms, vn_s, hist_s = even_mixer(hs, cache_conv[e], *prm)
            conv_p.append(hist_p)
            conv_s.append(hist_s)
            av_s.append(vn_s)
        else:
            o = layer // 2
            prm = (od_w_in[o], od_w_out[o], c_group_w[o], c_scale[o], d_rel_bias[o])
            zero_pool = jnp.zeros((xp.shape[0], POOL_HIST, C_WIDTH), xp.dtype)
            mp, np_p, k_p, v_p = odd_mixer(hp, zero_pool, pos_p, *prm, None, None)
            ms, np_s, k_s, v_s = odd_mixer(hs, cache_pool[o], pos_s, *prm, cache_k[o], cache_v[o])
            pool_p.append(np_p)
            pool_s.append(np_s)
            kp_l.append(k_p)
            vp_l.append(v_p)
            ks_l.append(k_s)
            vs_l.append(v_s)
        xp = xp + rms_norm(mp, norm_mix_post[layer])
        xs = xs + rms_norm(ms, norm_mix_post[layer])
        fp = swiglu(rms_norm(xp, norm_ffn_pre[layer]), ffn_w_gate[layer], ffn_w_up[layer], ffn_w_down[layer])
        fs = swiglu(rms_norm(xs, norm_ffn_pre[layer]), ffn_w_gate[layer], ffn_w_up[layer], ffn_w_down[layer])
        xp = xp + rms_norm(fp, norm_ffn_post[layer])
        xs = xs + rms_norm(fs, norm_ffn_post[layer])
    new_conv_prompt = jnp.stack(conv_p)
    new_pool_prompt = jnp.stack(pool_p)
    new_k_prompt = jnp.stack(kp_l)
    new_v_prompt = jnp.stack(vp_l)
    new_av_sample = jnp.stack(av_s)
    new_conv_sample = jnp.stack(conv_s)
    new_pool_sample = jnp.stack(pool_s)
    new_k_sample = jnp.stack(ks_l)
    new_v_sample = jnp.stack(vs_l)
    return (xp, xs, new_conv_prompt, new_pool_prompt, new_k_prompt, new_v_prompt,
            new_av_sample, new_conv_sample, new_pool_sample, new_k_sample, new_v_sample)
```

```python
import os
import numpy as np
import concourse.bass as bass
import concourse.mybir as mybir
from concourse.bass_utils import run_bass_kernel_spmd
from contextlib import ExitStack

F32 = mybir.dt.float32
BF16 = mybir.dt.bfloat16
AF = mybir.ActivationFunctionType
ALU = mybir.AluOpType

ENGS = ("pe", "act", "dve", "pool", "sp")


class _Op:
    __slots__ = ("eng", "fn", "deps", "signal", "dma", "sig")

    def __init__(self, eng, fn, deps, dma):
        self.eng = eng
        self.fn = fn
        self.deps = deps
        self.dma = dma
        self.signal = False
        self.sig = None


class Prog:
    def __init__(self, nc, n_dma_sems=16, same_engine_sync=True):
        self.nc = nc
        self.ops = []
        self.lastw = {}
        self.readers = {}
        self.n_dma_sems = n_dma_sems
        self.same_engine_sync = same_engine_sync

    def add(self, eng, fn, reads=(), writes=(), dma=False):
        idx = len(self.ops)
        deps = set()
        for k in reads:
            w = self.lastw.get(k)
            if w is not None:
                deps.add(w)
        for k in writes:
            w = self.lastw.get(k)
            if w is not None:
                deps.add(w)
            for r in self.readers.get(k, ()):
                deps.add(r)
        for k in reads:
            self.readers.setdefault(k, []).append(idx)
        for k in writes:
            self.lastw[k] = idx
            self.readers[k] = []
        deps.discard(idx)
        self.ops.append(_Op(eng, fn, deps, dma))
        return idx

    def dma(self, q, out, in_, reads=(), writes=(), **kw):
        return self.add(q, lambda e: e.dma_start(out=out, in_=in_, **kw), reads, writes, dma=True)

    def emit(self, stack):
        nc = self.nc
        ops = self.ops
        for i, op in enumerate(ops):
            keep = set()
            for d in op.deps:
                p = ops[d]
                if p.eng == op.eng and not p.dma and not op.dma:
                    if op.eng == "pe" or not self.same_engine_sync:
                        continue
                keep.add(d)
            newest = {}
            pruned = set()
            for d in keep:
                p = ops[d]
                if p.dma:
                    pruned.add(d)
                elif p.eng not in newest or newest[p.eng] < d:
                    newest[p.eng] = d
            pruned.update(newest.values())
            op.deps = pruned
            for d in pruned:
                ops[d].signal = True
        sems = {e: stack.enter_context(nc.semaphore("s_" + e)) for e in ENGS if e != "sp"}
        dsems = {}
        for q in ("sp", "pool", "act"):
            if any(o.dma and o.eng == q for o in ops):
                dsems[q] = [stack.enter_context(nc.semaphore("d_%s%d" % (q, j))) for j in range(self.n_dma_sems)]
        cnt = {e: 0 for e in ENGS}
        dcnt = {q: 0 for q in dsems}
        prewait = {}
        for i, op in enumerate(ops):
            if op.dma:
                n = dcnt[op.eng]
                dcnt[op.eng] += 1
                slot = n % self.n_dma_sems
                rnd = n // self.n_dma_sems
                op.sig = (dsems[op.eng][slot], 16 * (rnd + 1))
                if rnd > 0:
                    prewait[i] = (dsems[op.eng][slot], 16 * rnd)
            elif op.signal:
                cnt[op.eng] += 1
                op.sig = (sems[op.eng], cnt[op.eng])
        self.sig_counts = dict(cnt)
        per_eng = {e: [] for e in ENGS}
        for i, op in enumerate(ops):
            per_eng[op.eng].append(i)

        def run_engine(e, eng):
            waited = {}
            for i in per_eng[e]:
                op = ops[i]
                ws = [ops[d].sig for d in op.deps]
                if i in prewait:
                    ws.append(prewait[i])
                best = {}
                for (s, v) in ws:
                    key = id(s)
                    if key not in best or best[key][1] < v:
                        best[key] = (s, v)
                for key, (s, v) in best.items():
                    if waited.get(key, 0) < v:
                        eng.wait_ge(s, v)
                        waited[key] = v
                inst = op.fn(eng)
                if inst is None:
                    continue
                if op.dma:
                    inst.then_inc(op.sig[0], 16)
                elif op.signal:
                    inst.then_inc(op.sig[0], 1)

        with nc.Block() as block:
            if per_eng["sp"]:
                @block.sync
                def _(eng):
                    run_engine("sp", eng)
            if per_eng["pe"]:
                @block.tensor
                def _(eng):
                    run_engine("pe", eng)
            if per_eng["act"]:
                @block.scalar
                def _(eng):
                    run_engine("act", eng)
            if per_eng["dve"]:
                @block.vector
                def _(eng):
                    run_engine("dve", eng)
            if per_eng["pool"]:
                @block.gpsimd
                def _(eng):
                    run_engine("pool", eng)


NCORES = 8
D = 1024
SEQ = 2048
NP = 1024
TOK = 1056
FFN = 2816
KFF = 22
EPS = 1e-6
PG = 1056
NPAGES = 44
WO_PG = 36
WD_PG = 22

CFG = {"attn_biasmask": 3, "attn_hps": 4, "attn_nobias": False, "attn_chunks": None, "attn_parts": 4, "odd_attn": True, "odd_pool": True, "odd_sattn": True, "layers": 4, "layer_list": None, "mixer": True, "ffn": True, "same_engine_sync": True}


def build_program(cfg=None):
    cfg = dict(CFG, **(cfg or {}))
    NL = cfg["layers"]
    nc = bass.Bass("TRN2", target_bir_lowering=False)

    def din(name, shape):
        return nc.dram_tensor(name, list(shape), F32, kind="ExternalInput").ap()

    def dout(name, shape):
        return nc.dram_tensor(name, list(shape), F32, kind="ExternalOutput").ap()

    xp = din("xp", [SEQ, D])
    xs = din("xs", [32, D])
    cconv = din("cconv", [2, 2, 2, 512])
    cpool = din("cpool", [2, 2, 15, 512])
    ck = din("ck", [2, 2, 512, 512])
    cv = din("cv", [2, 2, 512, 512])
    norms = din("norms", [4, 4, D])
    wg = din("wg", [4, D, FFN])
    wu = din("wu", [4, D, FFN])
    wd = din("wd", [4, FFN, D])
    evin = din("evin", [2, D, 2560])
    evout = din("evout", [2, D, D])
    avg = din("avg", [2, 512])
    asw = din("asw", [2, 4, 128, 128])
    asb = din("asb", [2, 512])
    bcw = din("bcw", [2, 3, 512])
    odin = din("odin", [2, D, 2048])
    odout = din("odout", [2, D, D])
    cgw = din("cgw", [2, 4, 128, 128])
    csc = din("csc", [2, 512])
    drb = din("drb", [2, 8, 129])
    c_ident = din("c_ident", [128, 128])
    c_jmat = din("c_jmat", [128, 128])
    c_tri = din("c_tri", [128, 128])
    c_rcnt = din("c_rcnt", [128, 64])

    y_p = dout("y_p", [SEQ, D])
    y_s = dout("y_s", [32, D])
    o_conv_p = dout("o_conv_p", [2, 2, 512])
    o_pool_p = dout("o_pool_p", [2, 15, 512])
    o_k_p = dout("o_k_p", [2, 512, 512])
    o_v_p = dout("o_v_p", [2, 512, 512])
    o_av_s = dout("o_av_s", [2, 32, 512])
    o_conv_s = dout("o_conv_s", [2, 2, 2, 512])
    o_pool_s = dout("o_pool_s", [2, 2, 15, 512])
    o_k_s = dout("o_k_s", [2, 32, 512])
    o_v_s = dout("o_v_s", [2, 32, 512])
    OUT_KEYS = []

    kt_stash = [nc.dram_tensor("kt_stash%d" % o, [128, 4 * 512], BF16, kind="Internal").ap() for o in range(2)]
    v_stash = [nc.dram_tensor("v_stash%d" % o, [128, 4 * 520], BF16, kind="Internal").ap() for o in range(2)]
    rb_ext = [nc.dram_tensor("rb_ext%d" % o, [8, 256], F32, kind="Internal").ap() for o in range(2)]

    with ExitStack() as st:
        def sb(name, shape, dt=F32):
            return st.enter_context(nc.sbuf_tensor(name, list(shape), dt))

        P = Prog(nc, same_engine_sync=cfg["same_engine_sync"])

        def mm(out, lhsT, rhs, start, stop, r, w):
            P.add("pe", lambda e: e.matmul(out, lhsT=lhsT, rhs=rhs, start=start, stop=stop), r, w)

        def tr(out, in_, ident_, r, w):
            P.add("pe", lambda e: e.transpose(out=out, in_=in_, identity=ident_), r, w)

        def act(out, in_, func, r, w, **kw):
            P.add("act", lambda e: e.activation(out=out, in_=in_, func=func, **kw), r, w)

        def tt(out, in0, in1, op, r, w, eng="dve"):
            P.add(eng, lambda e: e.tensor_tensor(out=out, in0=in0, in1=in1, op=op), r, w)

        def ts(out, in0, s1, s2, op0, op1, r, w, eng="dve"):
            if op1 is None:
                P.add(eng, lambda e: e.tensor_scalar(out=out, in0=in0, scalar1=s1, scalar2=None, op0=op0), r, w)
            else:
                P.add(eng, lambda e: e.tensor_scalar(out=out, in0=in0, scalar1=s1, scalar2=s2, op0=op0, op1=op1), r, w)

        def stt(out, in0, scalar, in1, op0, op1, r, w):
            P.add("dve", lambda e: e.scalar_tensor_tensor(out=out, in0=in0, scalar=scalar, in1=in1, op0=op0, op1=op1), r, w)

        def cp(out, in_, r, w, eng="dve"):
            P.add(eng, lambda e: e.tensor_copy(out=out, in_=in_), r, w)

        def mset(ap, val, w, eng="dve"):
            P.add(eng, lambda e: e.memset(ap, val), (), w)

        def dma(out, in_, r, w, q="sp"):
            P.dma(q, out, in_, r, w)

        X = sb("X", [128, 8, TOK])
        HY = sb("HY", [128, 8, TOK], BF16)
        FB = sb("FB", [128, 8, 512])
        AR = sb("AR", [128, NPAGES * PG], BF16)
        NWST = 5
        WST = [sb("WST%d" % i, [128, 8, 256], BF16) for i in range(NWST)]
        SQ = [sb("SQ%d" % i, [128, 512], BF16) for i in range(4)]
        SG = [sb("SG%d" % i, [128, 512]) for i in range(2)]
        FBf = FB[:].rearrange("p c t -> p (c t)")
        XS = [FBf[:, 0:1024], FBf[:, 1024:2048]]
        KXS = [[("FB", 0), ("FB", 1)], [("FB", 2), ("FB", 3)]]
        SMALL = FBf[:, 3072:4096]
        KSM = [("FB", 6), ("FB", 7)]
        ident = sb("ident", [128, 128])
        identb = sb("identb", [128, 128], BF16)
        jb = sb("jb", [128, 128], BF16)
        onesb = sb("onesb", [128, 128], BF16)
        rcnt = sb("rcnt", [128, 4, 16])
        epsc = sb("epsc", [128, 1])
        GC = sb("GC", [128, 128])
        VC = sb("VC", [128, 32])
        VGB = sb("VGB", [128, 512])
        WST_T = [sb("WsT%d" % e, [128, 4, 128], BF16) for e in range(2)]
        BSH = sb("BSH", [1, 512], BF16)
        BSL = sb("BSL", [1, 512], BF16)
        WGB = [sb("WGB%d" % o, [128, 4, 128], BF16) for o in range(2)]
        TH65 = [sb("TH65_%d" % i, [128, 8, 64], BF16) for i in range(2)]
        TH1 = [sb("TH1_%d" % i, [128, 8, 64], BF16) for i in range(2)]
        THB = [sb("THB_%d" % i, [128, 8, 64], BF16) for i in range(2)]
        THC = [sb("THC_%d" % i, [128, 8, 16], BF16) for i in range(2)]
        CFAR = sb("CFAR", [128, 8])
        ZH = sb("ZH", [128, 2, 4, 2])
        PH = sb("PH", [128, 2, 4, 15])
        ZF = sb("ZF", [128, 4, 3, 2])
        ZSb = sb("ZSb", [128, 4, 2, 18], BF16)
        PS = sb("PS", [128, 4, 2, 32])
        PST = [sb("PST%d" % i, [128, 2, 32]) for i in range(2)]
        BN6 = [sb("BN6_%d" % i, [128, 6]) for i in range(3)]
        MV = [sb("MV%d" % i, [128, 2]) for i in range(3)]
        RS = [sb("RS%d" % i, [128, 2]) for i in range(3)]
        REC = [sb("REC%d" % i, [64, 8]) for i in range(2)]
        VSN = sb("VSN", [16, 2, 8, 65], BF16)
        KSN = sb("KSN", [128, 4, 32], BF16)
        ps = [st.enter_context(nc.psum_tensor("ps%d" % i, [128, 512], F32)) for i in range(8)]

        pools = {"mm": [0, 1, 2, 3], "aux": [4, 5], "st": [6, 7]}
        pool_ctr = {k: 0 for k in pools}

        def psget(pool):
            lst = pools[pool]
            i = lst[pool_ctr[pool] % len(lst)]
            pool_ctr[pool] += 1
            return ps[i], ("ps", i)

        def pg(p0, n=1):
            return AR[:, p0 * PG:(p0 + n) * PG]

        def kpg(p0, n=1):
            return [("AR", p) for p in range(p0, p0 + n)]

        A_v = pg(0, 22).rearrange("p (k t) -> p k t", k=22)
        WD_v = pg(WD_PG, 22).rearrange("p (k t) -> p k t", k=22)
        WO_v = pg(WO_PG, 8).rearrange("p (k t) -> p k t", k=8)
        YE_v = pg(27, 8).rearrange("p (k t) -> p k t", k=8)

        def tiles_of(ps_):
            t = [(0, 512), (512, 512)]
            if ps_ == 1:
                t.append((1024, 32))
            return t

        dma(ident[:], c_ident, [], ["ident"])
        dma(rcnt[:].rearrange("p g n -> p (g n)"), c_rcnt, [], ["rcnt"])
        cp(identb[:], ident[:], ["ident"], ["identb"])
        jf = XS[1][:, 0:128]
        dma(jf, c_jmat, [], KXS[1])
        cp(jb[:], jf, KXS[1], ["jb"])
        mset(onesb[:], 1.0, ["onesb"])
        mset(epsc[:], EPS, ["epsc"])
        GROW = XS[1][:, 128:256]
        VROW = XS[1][0:32, 256:384]
        tri = XS[1][:, 384:512]
        dma(GROW, norms.rearrange("t l (c p) -> (t l c) p", p=128), [], KXS[1])
        dma(VROW[0:24, :], bcw.rearrange("e k (c p) -> (e k c) p", p=128), [], KXS[1])
        dma(VROW[24:32, :], csc.rearrange("o (g p) -> (o g) p", p=128), [], KXS[1])
        dma(tri, c_tri, [], KXS[1])
        b_, kb = psget("aux")
        tr(b_[:, 0:128], GROW, ident[:], KXS[1] + ["ident"], [kb])
        act(GC[:], b_[:, 0:128], AF.Copy, [kb], ["GC"])
        b_, kb = psget("aux")
        tr(b_[:, 0:32], VROW, ident[0:32, 0:32], KXS[1] + ["ident"], [kb])
        act(VC[:], b_[:, 0:32], AF.Copy, [kb], ["VC"])

        def gcol(t, l, c):
            i = (t * 4 + l) * 8 + c
            return GC[:, i:i + 1]

        def convw(e, k, c):
            i = (e * 3 + k) * 4 + c
            return VC[:, i:i + 1]

        def cscale(o, g):
            i = 24 + o * 4 + g
            return VC[:, i:i + 1]

        for e in range(2):
            dma(XS[0][:, 0:512].rearrange("p (g j) -> p g j", g=4), asw[e].rearrange("g i j -> i g j"), [], KXS[0])
            b_, kb = psget("aux")
            for g in range(4):
                tr(b_[:, g * 128:(g + 1) * 128], XS[0][:, g * 128:(g + 1) * 128], ident[:], KXS[0] + ["ident"], [kb])
            tt(WST_T[e][:], b_[:].rearrange("p (g i) -> p g i", g=4), tri[:, None, :].to_broadcast([128, 4, 128]), ALU.mult,
               [kb] + KXS[1], [("WsT", e)])
        for o in range(2):
            dma(WGB[o][:], cgw[o].rearrange("g c d -> c g d"), [], [("WGB", o)], q="pool")
            RBS = XS[1][0:8, 512:768]
            dma(RBS[:, 0:129], drb[o], [], KXS[1])
            cp(RBS[:, 129:256], RBS[:, 128:129].to_broadcast([8, 127]), KXS[1], KXS[1])
            dma(rb_ext[o], RBS, KXS[1], [("rbx", o)])
        odd_built = [False]

        def build_even_consts(e):
            dma(VGB[:], bass.AP(avg.tensor, e * 512, [[0, 128], [1, 512]]), [], ["VGB"])
            dma(SMALL[0:1, 0:512], asb[e:e + 1, :], [], KSM)
            cp(BSH[:], SMALL[0:1, 0:512], KSM, ["BSH"])
            cp(SMALL[0:1, 512:1024], BSH[:], ["BSH"], KSM)
            tt(BSL[:], SMALL[0:1, 0:512], SMALL[0:1, 512:1024], ALU.subtract, KSM, ["BSL"])

        def build_odd_consts(o):
            for h in range(8):
                dma(CFAR[:, h:h + 1], bass.AP(drb.tensor, (o * 8 + h) * 129 + 128, [[0, 128], [1, 1]]), [], ["CFAR"])
            for (TH, kh, kl, off, r0, cols) in ((TH65[o], ("TH65h", o), "TH65l", 65, 0, 64), (TH1[o], ("TH1h", o), "TH1l", 1, 0, 64),
                                                (THB[o], ("THBh", o), "THBl", 1, 64, 64), (THC[o], ("THCh", o), "THCl", 49, 112, 16)):
                rows = 128
                src = bass.AP(rb_ext[o].tensor, off, [[1, 128 - r0], [256, 8], [1, cols]])
                tmp = SMALL[0:rows, 0:8 * cols].rearrange("p (h q) -> p h q", h=8)
                tmp2 = SMALL[0:rows, 512:512 + 8 * cols].rearrange("p (h q) -> p h q", h=8)
                if r0 > 0:
                    mset(tmp, 0.0, KSM)
                dma(tmp[r0:128], src, [("rbx", o)], KSM)
                tt(tmp, tmp, CFAR[0:rows, :, None].to_broadcast([rows, 8, cols]), ALU.subtract, KSM + ["CFAR"], KSM)
                ts(TH[:], tmp, 8.0, None, ALU.mult, None, KSM, [kh])

        wst_ctr = [0]

        def load_stage(Wsrc, col0):
            i = wst_ctr[0] % NWST
            wst_ctr[0] += 1
            src = Wsrc[:, col0:col0 + 256].rearrange("(c p) n -> p c n", p=128)
            dma(WST[i][:], src, [], [("WST", i)], q="pool")
            return i

        def stats_to_rstd(sbank, skey, n):
            act(sbank[:, 0:n], sbank[:, 0:n], AF.Ln, [skey, "epsc"], [skey], scale=1.0 / D, bias=epsc[:])
            act(sbank[:, 0:n], sbank[:, 0:n], AF.Exp, [skey], [skey], scale=-0.5)

        sq_ctr = [0]

        def sq_get():
            i = sq_ctr[0] % 4
            sq_ctr[0] += 1
            return SQ[i], ("SQ", i)

        def pre_norm(ntype, l, tiles):
            for (t0, n) in tiles:
                pre_norm_tile(ntype, l, t0, n)

        def pre_norm_tile(ntype, l, t0, n, part=None):
            if part in (None, "A"):
                if n <= 64:
                    act(HY[:, :, t0:t0 + n], X[:, :, t0:t0 + n], AF.Square, [("X", c, t0) for c in range(8)], [("HY", c, t0) for c in range(8)])
                else:
                    for c in range(8):
                        act(HY[:, c, t0:t0 + n], X[:, c, t0:t0 + n], AF.Square, [("X", c, t0)], [("HY", c, t0)])
            if part in (None, "B"):
                sbank, skey = psget("st")
                for c in range(8):
                    mm(sbank[:, 0:n], onesb[:], HY[:, c, t0:t0 + n], c == 0, c == 7, [("HY", c, t0), "onesb"], [skey])
                stats_to_rstd(sbank, skey, n)
                for c in range(8):
                    stt(HY[:, c, t0:t0 + n], X[:, c, t0:t0 + n], gcol(ntype, l, c), sbank[:, 0:n], ALU.mult, ALU.mult,
                        [("X", c, t0), skey, "GC"], [("HY", c, t0)])

        def dense_out_residual(Wv, wkeys, KC, src_fn, src_keys_fn, ntype, l, tiles, after_tile=None):
            run_deferred()
            prev_tile = None
            for (t0, n) in tiles:
                sbank, skey = psget("st")
                pendq = []
                if n <= 64:
                    b_, kb = psget("mm")
                    for nb in range(8):
                        for k in range(KC):
                            mm(b_[:, nb * n:(nb + 1) * n], Wv[:, k, nb * 128:(nb + 1) * 128], src_fn(k, t0, n), k == 0, k == KC - 1,
                               wkeys + src_keys_fn(k, t0), [kb])
                    if after_tile is not None and prev_tile is not None:
                        after_tile(*prev_tile)
                        prev_tile = None
                    s_, sk = sq_get()
                    act(s_[:, 0:8 * n], b_[:, 0:8 * n], AF.Square, [kb], [sk])
                    i0 = (ntype * 4 + l) * 8
                    fbv = FB[:, :, 0:n]
                    kfb = [("FB", nb) for nb in range(8)]
                    tt(fbv, b_[:, 0:8 * n].rearrange("p (c t) -> p c t", c=8), GC[:, i0:i0 + 8, None].to_broadcast([128, 8, n]), ALU.mult,
                       [kb, "GC"], kfb)
                    for nb in range(8):
                        mm(sbank[:, 0:n], onesb[:], s_[:, nb * n:(nb + 1) * n], nb == 0, nb == 7, [sk, "onesb"], [skey])
                    stats_to_rstd(sbank, skey, n)
                    tt(fbv, fbv, sbank[:, None, 0:n].to_broadcast([128, 8, n]), ALU.mult, kfb + [skey], kfb)
                    xk = [("X", nb, t0) for nb in range(8)]
                    tt(X[:, :, t0:t0 + n], X[:, :, t0:t0 + n], fbv, ALU.add, kfb + xk, xk)
                    prev_tile = (t0, n)
                    continue
                for nb in range(8):
                    b_, kb = psget("mm")
                    for k in range(KC):
                        mm(b_[:, 0:n], Wv[:, k, nb * 128:(nb + 1) * 128], src_fn(k, t0, n), k == 0, k == KC - 1,
                           wkeys + src_keys_fn(k, t0), [kb])
                    if len(pendq) > 1:
                        pendq.pop(0)()
                    if after_tile is not None and prev_tile is not None and nb == (3 if KC > 8 else 5):
                        after_tile(*prev_tile, part="A")
                    if after_tile is not None and prev_tile is not None and nb == (5 if KC > 8 else 7):
                        after_tile(*prev_tile, part="B")
                        prev_tile = None
                    s_, sk = sq_get()
                    act(s_[:, 0:n], b_[:, 0:n], AF.Square, [kb], [sk])
                    act(FB[:, nb, 0:n], b_[:, 0:n], AF.Copy, [kb, "GC"], [("FB", nb)], scale=gcol(ntype, l, nb))

                    def pend(s_=s_, sk=sk, nb=nb):
                        mm(sbank[:, 0:n], onesb[:], s_[:, 0:n], nb == 0, nb == 7, [sk, "onesb"], [skey])
                    pendq.append(pend)
                while pendq:
                    pendq.pop(0)()
                stats_to_rstd(sbank, skey, n)
                for nb in range(8):
                    tt(FB[:, nb, 0:n], FB[:, nb, 0:n], sbank[:, 0:n], ALU.mult, [("FB", nb), skey], [("FB", nb)])
                    tt(X[:, nb, t0:t0 + n], X[:, nb, t0:t0 + n], FB[:, nb, 0:n], ALU.add, [("FB", nb), ("X", nb, t0)], [("X", nb, t0)])
                if after_tile is not None and prev_tile is not None:
                    after_tile(*prev_tile)
                prev_tile = (t0, n)
            if after_tile is not None and prev_tile is not None:
                deferred.append(lambda part=None, pt=prev_tile: after_tile(*pt, part=part))

        deferred = []

        deferred_a = [False]

        def run_deferred(part=None):
            if part == "A":
                for f_ in deferred:
                    f_("A")
                deferred_a[0] = bool(deferred)
                return
            while deferred:
                deferred.pop(0)("B" if deferred_a[0] else None)
            deferred_a[0] = False

        def two_stage_split(loaders, body, tiles):
            stgs = [ld() for ld in loaders]
            for s_, stg in enumerate(stgs):
                for (t0, n) in tiles[:-1]:
                    body(s_, stg, t0, n)
                if s_ == 0:
                    run_deferred("A")
            run_deferred()
            for s_, stg in enumerate(stgs):
                body(s_, stg, *tiles[-1])

        def resident_thunks(Wsrc, KC, page0):
            th = []
            for k in range(0, KC, 2):
                def f(k=k):
                    dst = pg(page0 + k, 2).rearrange("p (k t) -> p k t", k=2)[:, :, 0:1024]
                    src = Wsrc[k * 128:(k + 2) * 128, :].rearrange("(k p) n -> p k n", p=128)
                    dma(dst, src, [], kpg(page0 + k, 2), q="pool")
                th.append(f)
            return th

        def fm_block(stage, jj, t0, n):
            b_, kb = psget("mm")
            for k in range(8):
                mm(b_[:, 0:n], WST[stage][:, k, jj * 128:(jj + 1) * 128], HY[:, k, t0:t0 + n], k == 0, k == 7,
                   [("WST", stage), ("HY", k, t0)], [kb])
            return b_, kb

        def tm_block(stages, t0, m):
            b_, kb = psget("mm")
            for half, stg in enumerate(stages):
                for k in range(8):
                    mm(b_[0:m, half * 256:(half + 1) * 256], HY[:, k, t0:t0 + m], WST[stg][:, k, :], k == 0, k == 7,
                       [("WST", stg), ("HY", k, (t0 // 512) * 512)], [kb])
            return b_, kb

        def even_mixer(ps_, l, tiles):
            e = l // 2
            Wsrc = evin[e]
            XB = [pg(2 * c, 2).bitcast(F32) for c in range(4)]
            kXB = [kpg(2 * c, 2) for c in range(4)]
            Z = [pg(8 + 2 * c, 2).bitcast(F32) for c in range(4)]
            kZ = [kpg(8 + 2 * c, 2) for c in range(4)]
            T1 = [pg(17 + i).bitcast(F32) for i in range(2)]
            kT1 = [kpg(17 + i) for i in range(2)]
            SSB = pg(19, 4).rearrange("p (g t) -> p g t", g=4)
            kSSB = kpg(19, 4)
            VNt = [pg(p_)[:, 0:512] for p_ in (23, 16, 35)]
            kVNl = [kpg(p_) for p_ in (23, 16, 35)]
            LT = [pg(24 + i).bitcast(F32)[:, 0:512] for i in range(3)]
            kLT = [kpg(24 + i) for i in range(3)]
            wo_th = resident_thunks(evout[e], 8, WO_PG)
            def xb_body(s, stg, t0, n):
                for jj in range(2):
                    c = 2 * s + jj
                    b_, kb = fm_block(stg, jj, t0, n)
                    act(XB[c][:, t0:t0 + n], b_[:, 0:n], AF.Copy, [kb], kXB[c])
            two_stage_split([lambda s=s: load_stage(Wsrc, 2048 + s * 256) for s in range(2)], xb_body, tiles)
            build_even_consts(e)
            if not odd_built[0]:
                odd_built[0] = True
                for o_ in range(2):
                    build_odd_consts(o_)
            Zb = pg(8, 5)[:, 0:4 * 1058].rearrange("p (c t) -> p c t", c=4)
            kZb = kpg(8, 5)
            DG = pg(13, 2)[:, 0:1536].rearrange("p (i j) -> p i j", i=12)
            kDG = kpg(13, 2)
            for c in range(4):
                for k in range(3):
                    ts(DG[:, c * 3 + k, :], identb[:], convw(e, k, c), None, ALU.mult, None, ["identb", "VC"], kDG)
            if ps_ == 0:
                mset(Zb[:, :, 0:2], 0.0, kZb)
            else:
                cp(Zb[:, :, 0:2], ZH[:, e], [("ZH", e)], kZb)
            for s in range(2):
                stg = load_stage(Wsrc, 1536 + s * 256)
                if s == 1:
                    for f_ in wo_th:
                        f_()
                for (t0, n) in tiles:
                    for jj in range(2):
                        c = 2 * s + jj
                        b_, kb = fm_block(stg, jj, t0, n)
                        tt(Zb[:, c, 2 + t0:2 + t0 + n], b_[:, 0:n], XB[c][:, t0:t0 + n], ALU.mult, [kb] + kXB[c], kZb)
                        if t0 == 512:
                            tt(ZF[:, c, 0, :], b_[:, 510:512], XB[c][:, 1022:1024], ALU.mult, [kb] + kXB[c], [("ZF", c)])
                        if t0 == 1024:
                            tt(ZF[:, c, 1, :], b_[:, 14:16], XB[c][:, 1038:1040], ALU.mult, [kb] + kXB[c], [("ZF", c)])
                            tt(ZF[:, c, 2, :], b_[:, 30:32], XB[c][:, 1054:1056], ALU.mult, [kb] + kXB[c], [("ZF", c)])
            for c in range(4):
                for (t0, n) in tiles[:2]:
                    b_, kb = psget("mm")
                    for k in range(3):
                        mm(b_[:, 0:n], DG[:, c * 3 + k, :], Zb[:, c, t0 + k:t0 + k + n], k == 0, k == 2, kDG + kZb, [kb])
                    act(XB[c][:, t0:t0 + n], b_[:, 0:n], AF.Copy, [kb], kXB[c])
            if ps_ == 0:
                for c in range(4):
                    cp(ZH[:, e, c, :], ZF[:, c, 0, :], [("ZF", c)], [("ZH", e)])
            if ps_ == 1:
                b_, kb = psget("aux")
                for c in range(4):
                    tr(b_[0:2, c * 128:(c + 1) * 128], ZF[:, c, 0, :], ident[:], [("ZF", c), "ident"], [kb])
                act(SMALL[0:2, 0:512], b_[0:2, 0:512], AF.Copy, [kb], KSM)
                dma(o_conv_p[e], SMALL[0:2, 0:512], KSM, [("o_conv_p", e)])
                OUT_KEYS.append(("o_conv_p", e))
                dma(SMALL[0:4, 512:1024], cconv[e].rearrange("s k n -> (s k) n"), [], KSM)
                b_, kb = psget("aux")
                for c in range(4):
                    tr(b_[:, c * 4:(c + 1) * 4], SMALL[0:4, 512 + c * 128:512 + (c + 1) * 128], ident[0:4, 0:4], KSM + ["ident"], [kb])
                for c in range(4):
                    act(ZSb[:, c, :, 0:2], b_[:, c * 4:(c + 1) * 4].rearrange("p (s k) -> p s k", s=2), AF.Copy, [kb], [("ZSb", c)])
                    cp(ZSb[:, c, :, 2:18], Zb[:, c, 1026:1058].rearrange("p (s t) -> p s t", s=2), kZb, [("ZSb", c)])
                    b2, kb2 = psget("mm")
                    for k in range(3):
                        mm(b2[:, 0:32].rearrange("p (s t) -> p s t", s=2), DG[:, c * 3 + k, :], ZSb[:, c, :, k:k + 16], k == 0, k == 2,
                           kDG + [("ZSb", c)], [kb2])
                    act(XB[c][:, 1024:1056], b2[:, 0:32], AF.Copy, [kb2], kXB[c])
                for s in range(2):
                    b_, kb = psget("aux")
                    for c in range(4):
                        tr(b_[0:2, c * 128:(c + 1) * 128], ZF[:, c, 1 + s, :], ident[:], [("ZF", c), "ident"], [kb])
                    act(SMALL[0:2, 0:512], b_[0:2, 0:512], AF.Copy, [kb], KSM)
                    dma(o_conv_s[e, s], SMALL[0:2, 0:512], KSM, [("o_conv_s", e, s)])
                    OUT_KEYS.append(("o_conv_s", e, s))
            for s in range(2):
                stg = load_stage(Wsrc, 1024 + s * 256)
                for (t0, n) in tiles:
                    for jj in range(2):
                        c = 2 * s + jj
                        b_, kb = fm_block(stg, jj, t0, n)
                        tt(YE_v[:, 4 + c, t0:t0 + n], b_[:, 0:n], XB[c][:, t0:t0 + n], ALU.mult, [kb] + kXB[c], kpg(27 + 4 + c))
            stg_v = [load_stage(Wsrc, 512), load_stage(Wsrc, 768)]
            chunks = [(ch * 128, 128, None) for ch in range(8)]
            if ps_ == 1:
                chunks += [(1024, 16, 0), (1040, 16, 1)]
            gate_pend = []
            for ci, (t0, m, sidx) in enumerate(chunks):
                b_, kb = tm_block(stg_v, t0, m)
                i2 = ci % 3
                kVN = kVNl[i2]
                P.add("dve", lambda e_, b_=b_, m=m, i2=i2: e_.bn_stats(out=BN6[i2][0:m, :], in_=b_[0:m, :]), [kb], [("BN6", i2)])
                P.add("dve", lambda e_, m=m, i2=i2: e_.bn_aggr(out=MV[i2][0:m, :], in_=BN6[i2][0:m, :]), [("BN6", i2)], [("MV", i2)])
                act(RS[i2][0:m, 0:1], MV[i2][0:m, 1:2], AF.Ln, [("MV", i2), "epsc"], [("RS", i2)], scale=1.0, bias=epsc[0:m, :])
                act(RS[i2][0:m, 1:2], RS[i2][0:m, 0:1], AF.Exp, [("RS", i2)], [("RS", i2)], scale=-0.5)
                ts(LT[i2][0:m, :], b_[0:m, :], MV[i2][0:m, 0:1], RS[i2][0:m, 1:2], ALU.subtract, ALU.mult, [kb, ("MV", i2), ("RS", i2)], kLT[i2])
                if sidx is None:
                    tt(VNt[i2][0:m, :], LT[i2][0:m, :], VGB[0:m, :], ALU.mult, kLT[i2] + ["VGB"], kVN)
                else:
                    tt(LT[i2][0:m, :], LT[i2][0:m, :], VGB[0:m, :], ALU.mult, kLT[i2] + ["VGB"], kLT[i2])
                    cp(VNt[i2][0:m, :], LT[i2][0:m, :], kLT[i2], kVN)
                    dma(o_av_s[e, sidx * 16:(sidx + 1) * 16, :], LT[i2][0:m, :], kLT[i2], [("o_av_s", e, sidx)])
                    OUT_KEYS.append(("o_av_s", e, sidx))
                def gate(t0=t0, m=m, i2=i2, kVN=kVN):
                    sbk, skb = psget("aux")
                    for g in range(4):
                        mm(sbk[:, g * 128:g * 128 + m], VNt[i2][0:m, g * 128:(g + 1) * 128], WST_T[e][0:m, g, 0:m], True, False,
                           kVN + [("WsT", e)], [skb])
                        mm(sbk[:, g * 128:g * 128 + m], onesb[0:1, :], BSH[0:1, g * 128:g * 128 + m], False, False, ["onesb", "BSH"], [skb])
                        mm(sbk[:, g * 128:g * 128 + m], onesb[0:1, :], BSL[0:1, g * 128:g * 128 + m], False, True, ["onesb", "BSL"], [skb])
                    act(SSB[:, :, t0:t0 + m], sbk[:].rearrange("p (g i) -> p g i", g=4)[:, :, 0:m], AF.Copy, [skb], kSSB)
                gate_pend.append(gate)
                if len(gate_pend) > 2:
                    gate_pend.pop(0)()
            while gate_pend:
                gate_pend.pop(0)()
            for s in range(2):
                stg = load_stage(Wsrc, s * 256)
                for (t0, n) in tiles:
                    for jj in range(2):
                        c = 2 * s + jj
                        b_, kb = fm_block(stg, jj, t0, n)
                        tt(YE_v[:, c, t0:t0 + n], b_[:, 0:n], SSB[:, c, t0:t0 + n], ALU.mult, [kb] + kSSB, kpg(27 + c))

        def odd_mixer(ps_, l, tiles):
            o = l // 2
            if not odd_built[0]:
                odd_built[0] = True
                for o_ in range(2):
                    build_odd_consts(o_)
            Wsrc = odin[o]
            Pb = [pg(2 * g, 2).bitcast(F32) for g in range(4)]
            kP = [kpg(2 * g, 2) for g in range(4)]
            TA = pg(8).bitcast(F32)
            TB = pg(9).bitcast(F32)
            kTA, kTB = kpg(8), kpg(9)
            POOLED = pg(10, 2).rearrange("p (g t) -> p g t", g=4)[:, :, 0:512]
            kPOOLED = kpg(10, 2)
            QTZ = pg(13, 8).rearrange("p (h t) -> p h t", h=8)
            kQT = kpg(13, 8)
            KTw = pg(21, 6)[:, 0:4 * 1568].rearrange("p (h t) -> p h t", h=4)
            kKT = kpg(21, 6)
            Vw = pg(27, 6)[:, 0:12 * 520].rearrange("p (m h d) -> p m h d", m=12, h=8)
            kV = kpg(27, 6)
            PT = [pg(30 + i)[:, 0:640].rearrange("p (a s q) -> p a s q", a=2, s=5) for i in range(2)]
            kPT = [kpg(30 + i) for i in range(2)]
            YD = [pg(32)[0:64, i * 512:(i + 1) * 512] for i in range(2)]
            kYD = kpg(32)
            KTMs = [pg(p_).bitcast(F32)[:, 0:512] for p_ in (0, 1, 2, 3)]
            kKTM = [kpg(p_) for p_ in (0, 1, 2, 3)]
            KFv = FB[:, 0:4, :]
            VF = [FB[:, 4 + i, :] for i in range(2)]
            wo_th = resident_thunks(odout[o], 8, WO_PG)
            def k_body(s, stg, t0, n):
                if True:
                    for jj in range(2):
                        hp = 2 * s + jj
                        b_, kb = fm_block(stg, jj, t0, n)
                        if t0 < 1024:
                            act(KTw[:, hp, 512 + t0:512 + t0 + n], b_[:, 0:n], AF.Copy, [kb], kKT)
                            if ps_ == 1 and t0 == 512:
                                act(KFv[:, hp, :], b_[:, 0:n], AF.Copy, [kb], [("FB", hp)])
                        else:
                            act(KSN[:, hp, :], b_[:, 0:n], AF.Copy, [kb], [("KSN", hp)])
                            act(KFv[:, hp, 0:32], b_[:, 0:n], AF.Copy, [kb], [("FB", hp)])
                            b2, kb2 = psget("aux")
                            tr(b2[0:32, 0:128], KFv[:, hp, 0:32], ident[:], [("FB", hp), "ident"], [kb2])
                            act(SMALL[0:32, hp * 128:(hp + 1) * 128], b2[0:32, 0:128], AF.Copy, [kb2], KSM)
                    if ps_ == 1 and t0 == 512:
                        for jj in range(2):
                            hp = 2 * s + jj
                            for ch in range(4):
                                b2, kb2 = psget("aux")
                                tr(b2[:, 0:128], KFv[:, hp, ch * 128:(ch + 1) * 128], ident[:], [("FB", hp), "ident"], [kb2])
                                act(KTMs[ch][:, hp * 128:(hp + 1) * 128], b2[:, 0:128], AF.Copy, [kb2], kKTM[ch])
            two_stage_split([lambda s=s: load_stage(Wsrc, 1024 + s * 256) for s in range(2)], k_body, tiles)
            if ps_ == 1:
                for ch in range(4):
                    dma(o_k_p[o, ch * 128:(ch + 1) * 128, :], KTMs[ch], kKTM[ch], [("o_k_p", o, ch)])
                    OUT_KEYS.append(("o_k_p", o, ch))
                dma(o_k_s[o], SMALL[0:32, 0:512], KSM, [("o_k_s", o)])
                OUT_KEYS.append(("o_k_s", o))
            if ps_ == 0:
                mset(KTw[:, :, 0:512], 0.0, kKT)
                mset(Vw[:, 0:4], 0.0, kV)
                mset(PH[:, o], 0.0, [("PH", o)])
            else:
                dma(KTw[:, :, 0:512], kt_stash[o].rearrange("p (h t) -> p h t", h=4), [("kt_stash", o)], kKT)
                dma(Vw[:, 0:4], v_stash[o].rearrange("p (m h d) -> p m h d", m=4, h=8), [("v_stash", o)], kV)
            mset(Vw[:, 4:12, :, 64:65], 1.0, kV)
            QTZp = pg(13, 8).rearrange("p (h a t) -> p h a t", h=4, a=2)
            mset(QTZp[64:128, :, 0, :], 0.0, kQT)
            mset(QTZp[0:64, :, 1, :], 0.0, kQT)
            for s in range(2):
                stg = load_stage(Wsrc, s * 256)
                for (t0, n) in tiles:
                    for jj in range(2):
                        g = 2 * s + jj
                        b_, kb = fm_block(stg, jj, t0, n)
                        act(Pb[g][:, t0:t0 + n], b_[:, 0:n], AF.Copy, [kb], kP[g])

            PBs = [pg(10, 2).rearrange("p (g t) -> p g t", g=4)[:, :, 0:512], pg(33, 2).rearrange("p (g t) -> p g t", g=4)[:, :, 0:512]]
            kPBs = [kpg(10, 2), kpg(33, 2)]
            pool_th = []

            def pool_thunk(ti, t0, n, g):
                pb = PBs[ti]
                kpb = kPBs[ti]
                win = 2 ** (g + 1)
                src_lo = t0 - (win - 2)
                cur, kcur = TB, kTB
                oth, koth = TA, kTA
                j0 = 16 - (win - 2)
                ln = n + (win - 2)
                if t0 == 0:
                    cp(TA[:, 1:16], PH[:, o, g, :], [("PH", o)], kTA)
                    cp(TA[:, 16:16 + n], Pb[g][:, 0:n], kP[g], kTA)
                    tt(TB[:, j0:j0 + ln], TA[:, j0:j0 + ln], TA[:, j0 - 1:j0 - 1 + ln], ALU.add, kTA, kTB)
                else:
                    tt(TB[:, j0:j0 + ln], Pb[g][:, src_lo:src_lo + ln], Pb[g][:, src_lo - 1:src_lo - 1 + ln], ALU.add, kP[g], kTB)
                for k in range(1, g + 1):
                    rem = win - 2 ** (k + 1)
                    j0 = 16 - rem
                    ln = n + rem
                    tt(oth[:, j0:j0 + ln], cur[:, j0:j0 + ln], cur[:, j0 - 2 ** k:j0 - 2 ** k + ln], ALU.add, kcur, koth)
                    cur, kcur, oth, koth = oth, koth, cur, kcur
                stt(pb[:, g, 0:n], cur[:, 16:16 + n], 1.0 / win, Pb[g][:, t0:t0 + n], ALU.mult, ALU.subtract, kcur + kP[g], kpb)
                if ps_ == 0 and t0 == 0:
                    tt(oth[:, 0:16], cur[:, 16:32], rcnt[:, g, :], ALU.mult, kcur + ["rcnt"], koth)
                    tt(pb[:, g, 0:16], oth[:, 0:16], Pb[g][:, 0:16], ALU.subtract, koth + kP[g], kpb)
            if cfg["odd_pool"]:
                for ti, (t0, n) in enumerate(tiles[:2]):
                    for g in range(4):
                        pool_th.append(lambda ti=ti, t0=t0, n=n, g=g: pool_thunk(ti, t0, n, g))
            stg_v = [load_stage(Wsrc, 1536), load_stage(Wsrc, 1792)]
            for f_ in wo_th:
                f_()
            chunks = [(ch * 128, 128, None) for ch in range(8)]
            if ps_ == 1:
                chunks += [(1024, 16, 0), (1040, 16, 1)]
            for ci, (t0, m, sidx) in enumerate(chunks):
                b_, kb = tm_block(stg_v, t0, m)
                bv = b_[0:m, :].rearrange("p (h d) -> p h d", h=8)
                if sidx is None:
                    act(Vw[:, 4 + ci, :, 0:64], bv, AF.Copy, [kb], kV)
                    if ps_ == 1 and ci >= 4:
                        i2 = ci % 2
                        act(VF[i2][:, :], b_[:, :], AF.Copy, [kb], [("FB", 4 + i2)])
                        dma(o_v_p[o, (ci - 4) * 128:(ci - 3) * 128, :], VF[i2][:, :], [("FB", 4 + i2)], [("o_v_p", o, ci)])
                        OUT_KEYS.append(("o_v_p", o, ci))
                else:
                    act(VSN[:, sidx, :, 0:64], bv, AF.Copy, [kb], [("VSN", sidx)])
                    i2 = sidx
                    act(VF[i2][0:16, :], b_[0:16, :], AF.Copy, [kb], [("FB", 4 + i2)])
                    dma(o_v_s[o, sidx * 16:(sidx + 1) * 16, :], VF[i2][0:16, :], [("FB", 4 + i2)], [("o_v_s", o, sidx)])
                    OUT_KEYS.append(("o_v_s", o, sidx))
                if pool_th:
                    pool_th.pop(0)()
            for s in range(2):
                stg = load_stage(Wsrc, 512 + s * 256)
                for (t0, n) in tiles:
                    for jj in range(2):
                        hp = 2 * s + jj
                        b_, kb = fm_block(stg, jj, t0, n)
                        act(QTZ[0:64, 2 * hp, t0:t0 + n], b_[0:64, 0:n], AF.Copy, [kb], kQT)
                        act(QTZ[64:128, 2 * hp + 1, t0:t0 + n], b_[64:128, 0:n], AF.Copy, [kb], kQT)

            while pool_th:
                pool_th.pop(0)()
            if cfg["odd_pool"]:
                for ti, (t0, n) in enumerate(tiles[:2]):
                    for g in range(4):
                        b_, kb = psget("mm")
                        mm(b_[:, 0:n], WGB[o][:, g, :], PBs[ti][:, g, 0:n], True, True, [("WGB", o)] + kPBs[ti], [kb])
                        act(HY[:, g, t0:t0 + n], b_[:, 0:n], AF.Copy, [kb, "VC"], [("HY", g, t0)], scale=cscale(o, g))
            for g in range(4):
                cp(PH[:, o, g, :], Pb[g][:, 1009:1024], kP[g], [("PH", o)])
            if ps_ == 1:
                b_, kb = psget("aux")
                for g in range(4):
                    tr(b_[0:15, g * 128:(g + 1) * 128], Pb[g][:, 1009:1024], ident[:], kP[g] + ["ident"], [kb])
                act(SMALL[0:15, 0:512], b_[0:15, 0:512], AF.Copy, [kb], KSM)
                dma(o_pool_p[o], SMALL[0:15, 0:512], KSM, [("o_pool_p", o)])
                OUT_KEYS.append(("o_pool_p", o))
                Pall = pg(0, 8).bitcast(F32).rearrange("p (g t) -> p g t", g=4)
                for s in range(2):
                    dma(SMALL[0:15, 512:1024], cpool[o, s], [], KSM)
                    b_, kb = psget("aux")
                    for g in range(4):
                        tr(b_[:, g * 16:g * 16 + 16], SMALL[0:16, 512 + g * 128:512 + (g + 1) * 128], ident[0:16, 0:16], KSM + ["ident"], [kb])
                    act(PS[:, :, s, 1:16], b_[:, 0:64].rearrange("p (g t) -> p g t", g=4)[:, :, 0:15], AF.Copy, [kb], [("PS", s)])
                    cp(PS[:, :, s, 16:32], Pall[:, :, 1024 + 16 * s:1040 + 16 * s], kpg(0, 8), [("PS", s)])
                    b3, kb3 = psget("aux")
                    for g in range(4):
                        tr(b3[0:15, g * 128:(g + 1) * 128], PS[:, g, s, 17:32], ident[:], [("PS", s), "ident"], [kb3])
                    act(SMALL[0:15, 0:512], b3[0:15, 0:512], AF.Copy, [kb3], KSM)
                    dma(o_pool_s[o, s], SMALL[0:15, 0:512], KSM, [("o_pool_s", o, s)])
                    OUT_KEYS.append(("o_pool_s", o, s))
                pbs = PBs[0]
                kpbs = kPBs[0]
                for g in range(4):
                    win = 2 ** (g + 1)
                    cur = PS[:, g]
                    kcur = [("PS", 0), ("PS", 1)]
                    for k in range(0, g + 1):
                        rem = win - 2 ** (k + 1)
                        j0 = 16 - rem
                        ln = 16 + rem
                        dst = PST[k % 2]
                        kdst = [("PST", k % 2)]
                        tt(dst[:, :, j0:j0 + ln], cur[:, :, j0:j0 + ln], cur[:, :, j0 - 2 ** k:j0 - 2 ** k + ln], ALU.add, kcur, kdst)
                        cur, kcur = dst, kdst
                    stt(pbs[:, g, 0:32].rearrange("p (s t) -> p s t", s=2), cur[:, :, 16:32], 1.0 / win, PS[:, g, :, 16:32], ALU.mult, ALU.subtract,
                        kcur + [("PS", 0), ("PS", 1)], kpbs)
                for g in range(4):
                    b_, kb = psget("mm")
                    mm(b_[:, 0:32], WGB[o][:, g, :], pbs[:, g, 0:32], True, True, [("WGB", o)] + kpbs, [kb])
                    act(HY[:, g, 1024:1056], b_[:, 0:32], AF.Copy, [kb, "VC"], [("HY", g, 1024)], scale=cscale(o, g))

            PTX = [pg(33 + i)[:, 0:1024].rearrange("p (a b s q) -> p a b s q", a=2, b=2, s=4) for i in range(2)]
            kPTX = [kpg(33 + i) for i in range(2)]
            PTZe = [pg(12)[:, i * 256:(i + 1) * 256].rearrange("p (h q) -> p h q", h=4) for i in range(2)]
            PTZo = [pg(12)[:, 512 + i * 256:512 + (i + 1) * 256].rearrange("p (h q) -> p h q", h=4) for i in range(2)]
            PTZs = [pg(35)[:, 512 + i * 256:512 + (i + 1) * 256].rearrange("p (h q) -> p h q", h=4) for i in range(2)]
            kPTZe = [[("PTZe", i)] for i in range(2)]
            kPTZo = [[("PTZo", i)] for i in range(2)]
            kPTZs = [[("PTZs", i)] for i in range(2)]
            kYD2 = [[("YD2", i)] for i in range(2)]
            FINE = kPTZe[0] + kPTZe[1] + kPTZo[0] + kPTZo[1] + kPTZs[0] + kPTZs[1] + kYD2[0] + kYD2[1]
            YD2 = [pg(35)[:, i * 256:(i + 1) * 256] for i in range(2)]
            mset(pg(12), 0.0, kpg(12) + FINE)
            mset(pg(35), 0.0, kpg(35) + FINE)
            SB = [[(ps[0], ("ps", 0)), (ps[1], ("ps", 1)), (ps[2], ("ps", 2))], [(ps[3], ("ps", 3)), (ps[4], ("ps", 4)), (ps[5], ("ps", 5))]]
            OB = [(ps[6], ("ps", 6)), (ps[7], ("ps", 7))]
            units = []

            def emit_S(k):
                U = units[k]
                nq = U["nq"]
                u = U["u"]
                st_ = SB[k % 2]
                b_, kb = st_[2]
                for hl in range(4):
                    hp = 2 * u + hl // 2
                    hh = hl % 2
                    qap, qk = U["qT"](hh, hp)
                    kap, kk = U["part"][2](hh, hp)
                    ebp = U["part"][4]
                    mm(b_[:, hl * 64:hl * 64 + nq], kap, qap, True, ebp is None, kk + qk, [kb])
                    if ebp is not None:
                        ebap, ebk = ebp(2 * hp + hh)
                        mm(b_[:, hl * 64:hl * 64 + nq], jb[:, :], ebap, False, True, ebk + ["jb"], [kb])
                for hpl in range(2):
                    hp = 2 * u + hpl
                    b_, kb = st_[hpl]
                    for hh in range(2):
                        qap, qk = U["qT"](hh, hp)
                        for si in range(4):
                            kap, kk = U["full"][si][0](hh, hp)
                            col = (hh * 4 + si) * 64
                            eb = U["full"][si][2]
                            mm(b_[:, col:col + nq], kap, qap, True, eb is None, kk + qk, [kb])
                            if eb is not None:
                                ebap, ebk = eb(2 * hp + hh)
                                mm(b_[:, col:col + nq], jb[:, :], ebap, False, True, ebk + ["jb"], [kb])

            def emit_E(k):
                U = units[k]
                nq = U["nq"]
                u = U["u"]
                st_ = SB[k % 2]
                ptx = PTX[k % 2]
                kptx = kPTX[k % 2]
                lo, hi, _, _, ebp, ptzl, kptz = U["part"]
                ptz = ptzl[k % 2]
                kptz = kptz[k % 2]
                b_, kb = st_[2]
                act(ptz[lo:hi, :, 0:nq], b_[lo:hi, 0:256].rearrange("p (h q) -> p h q", h=4)[:, :, 0:nq], AF.Exp, [kb], kptz, scale=0.125)
                for hpl in range(2):
                    hp = 2 * u + hpl
                    b_, kb = st_[hpl]
                    act(ptx[:, hpl, :, :, 0:nq], b_[:].rearrange("p (b s q) -> p b s q", b=2, s=4)[:, :, :, 0:nq], AF.Exp, [kb], kptx, scale=0.125)

            def emit_PV(k):
                U = units[k]
                nq = U["nq"]
                u = U["u"]
                ptx = PTX[k % 2]
                kptx = kPTX[k % 2]
                lo, hi, _, vpart, _, ptzl, kptz = U["part"]
                ptz = ptzl[k % 2]
                kptz = kptz[k % 2]
                ob, kob = OB[k % 2]
                for hl in range(4):
                    h = 4 * u + hl
                    for si in range(4):
                        vap, vk = U["full"][si][1](h)
                        mm(ob[0:nq, hl * 65:hl * 65 + 65], ptx[:, hl // 2, hl % 2, si, 0:nq], vap, si == 0, False, kptx + vk, [kob])
                    vap, vk = vpart(h)
                    mm(ob[0:nq, hl * 65:hl * 65 + 65], ptz[:, hl, 0:nq], vap, False, True, kptz + vk, [kob])
                ov = ob[0:nq, 0:260].rearrange("p (h d) -> p h d", h=4)
                rec = REC[k % 2]
                krec = ("REC", k % 2)
                P.add("dve", lambda e_, ov=ov, rec=rec, nq=nq: e_.reciprocal(out=rec[0:nq, 0:4], in_=ov[:, :, 64]), [kob], [krec])
                tt(YD2[k % 2][0:nq, :].rearrange("p (h d) -> p h d", h=4), ov[:, :, 0:64], rec[0:nq, 0:4, None].to_broadcast([nq, 4, 64]),
                   ALU.mult, [kob, krec], kYD2[k % 2])

            def emit_T(k):
                U = units[k]
                nq = U["nq"]
                u = U["u"]
                ob, kob = OB[k % 2]
                for hpl in range(2):
                    mm(ob[:, 384 + hpl * 64:384 + hpl * 64 + nq], YD2[k % 2][:, hpl * 128:(hpl + 1) * 128], identb[:, 0:nq], True, True,
                       kYD2[k % 2] + ["identb"], [kob])
                U["out"](u, ob[:, 384:512].rearrange("p (a q) -> p a q", a=2)[:, :, 0:nq], kob)

            def run_units():
                n = len(units)
                for k in range(n + 2):
                    if k < n:
                        emit_S(k)
                        emit_E(k)
                    if 1 <= k <= n:
                        emit_PV(k - 1)
                    if 2 <= k <= n + 1:
                        emit_T(k - 2)
                del units[:]

            def eb_full(THt, kTH, nq):
                return lambda h: (THt[:, h, 0:nq], [kTH])

            def samp_bufs():
                CK = pg(6, 2)[:, 0:2048].rearrange("p (m n) -> p m n", m=4)
                CV = pg(8, 2)[:, 0:2048].rearrange("p (m n) -> p m n", m=4)
                return CK, kpg(6, 2), CV, kpg(8, 2)

            def sample_prefetch(s_):
                CK, kCK, CV, kCV = samp_bufs()
                dma(CK, ck[o, s_].rearrange("(m p) n -> p m n", p=128), [], kCK, q="pool")
                dma(CV, cv[o, s_].rearrange("(m p) n -> p m n", p=128), [], kCV, q="pool")
            if ps_ == 1 and cfg["odd_sattn"]:
                sample_prefetch(0)
            for c in ((cfg["attn_chunks"] if cfg["attn_chunks"] is not None else range(16)) if cfg["odd_attn"] else []):
                m0 = c // 2

                def kblk(m):
                    return lambda hh, hp, m=m: (KTw[:, hp, m * 128:(m + 1) * 128], kKT)

                def vblk(m):
                    return lambda h, m=m: (Vw[:, m, h, :], kV)
                if c % 2 == 0:
                    full = [(kblk(m0 + i), vblk(m0 + i), eb_full(TH65[o], ("TH65h", o), 64) if i == 3 else None) for i in range(4)]
                    part = (0, 64, kblk(m0 + 4), vblk(m0 + 4), (lambda h: (THB[o][:, h, :], [("THBh", o)])), PTZe, kPTZe)
                else:
                    full = [(kblk(m0 + 1 + i), vblk(m0 + 1 + i), eb_full(TH1[o], ("TH1h", o), 64) if i == 3 else None) for i in range(4)]
                    part = (64, 128, kblk(m0), vblk(m0), None, PTZo, kPTZo)

                def qT_fn(hh, hp, c=c):
                    return QTZ[:, 2 * hp + hh, c * 64:(c + 1) * 64], kQT

                def out_fn(u, view, kob, c=c):
                    act(HY[:, 4 + 2 * u:6 + 2 * u, c * 64:(c + 1) * 64], view, AF.Copy, [kob],
                        [("HY", 4 + 2 * u + a_, (c // 8) * 512) for a_ in range(2)])
                for u in range(2):
                    units.append(dict(nq=64, u=u, qT=qT_fn, full=full, part=part, out=out_fn))
            run_units()
            if ps_ == 0:
                dma(kt_stash[o].rearrange("p (h t) -> p h t", h=4), KTw[:, :, 1024:1536], kKT, [("kt_stash", o)])
                dma(v_stash[o].rearrange("p (m h d) -> p m h d", m=4, h=8), Vw[:, 8:12], kV, [("v_stash", o)])

            if ps_ == 1 and cfg["odd_sattn"]:
                KTs = pg(0, 3)[:, 0:4 * 640].rearrange("p (h t) -> p h t", h=4)
                kKTs = kpg(0, 3)
                Vs = pg(3, 3)[:, 0:5 * 520].rearrange("p (m h d) -> p m h d", m=5, h=8)
                kVs = kpg(3, 3)
                for s in range(2):
                    CK, kCK, CV, kCV = samp_bufs()
                    mset(pg(35), 0.0, kpg(35) + FINE)
                    mset(KTs[:, :, 512:640], 0.0, kKTs)
                    for hp in range(4):
                        b2, kb2 = psget("aux")
                        b2v = b2[:].bitcast(BF16)
                        for m in range(4):
                            tr(b2v[:, m * 128:(m + 1) * 128], CK[:, m, hp * 128:(hp + 1) * 128], identb[:], kCK + ["identb"], [kb2])
                        act(KTs[:, hp, 0:512], b2v[:, 0:512], AF.Copy, [kb2], kKTs)
                        cp(KTs[:, hp, 512:528], KSN[:, hp, 16 * s:16 * s + 16], [("KSN", hp)], kKTs)
                    act(Vs[:, 0:4, :, 0:64], CV.rearrange("p m (h d) -> p m h d", h=8), AF.Copy, kCV, kVs)
                    mset(Vs[:, 0:4, :, 64:65], 1.0, kVs)
                    mset(Vs[:, 4], 0.0, kVs)
                    cp(Vs[0:16, 4, :, 0:64], VSN[:, s, :, 0:64], [("VSN", s)], kVs)
                    mset(Vs[0:16, 4, :, 64:65], 1.0, kVs)
                    if s == 0:
                        sample_prefetch(1)

                    def kblk_s(m):
                        return lambda hh, hp, m=m: (KTs[:, hp, m * 128:(m + 1) * 128], kKTs)

                    def vblk_s(m):
                        return lambda h, m=m: (Vs[:, m, h, :], kVs)
                    full = [(kblk_s(i), vblk_s(i), eb_full(TH65[o], ("TH65h", o), 16) if i == 3 else None) for i in range(4)]
                    part = (0, 16, kblk_s(4), vblk_s(4), (lambda h: (THC[o][:, h, :], [("THCh", o)])), PTZs, kPTZs)

                    def qT_fn(hh, hp, s=s):
                        return QTZ[:, 2 * hp + hh, 1024 + 16 * s:1040 + 16 * s], kQT

                    def out_fn(u, view, kob, s=s):
                        act(HY[:, 4 + 2 * u:6 + 2 * u, 1024 + 16 * s:1040 + 16 * s], view, AF.Copy, [kob],
                            [("HY", 4 + 2 * u + a_, 1024) for a_ in range(2)])
                    for u in range(2):
                        units.append(dict(nq=16, u=u, qT=qT_fn, full=full, part=part, out=out_fn))
                    run_units()
            P.add("dve", lambda e_: e_.memset(REC[0][0:1, 0:1], 0.0), FINE + [("REC", 0)], kpg(12) + kpg(35) + [("REC", 0)])

        for ps_ in range(2):
            tiles = tiles_of(ps_)
            for ch in range(8):
                xs_ = XS[ch % 2]
                kx = KXS[ch % 2]
                dma(xs_, xp[ps_ * NP + ch * 128:ps_ * NP + (ch + 1) * 128, :], [], kx)
                for half in range(2):
                    b_, kb = psget("mm")
                    for c4 in range(4):
                        c = half * 4 + c4
                        tr(b_[:, c4 * 128:(c4 + 1) * 128], xs_[:, c * 128:(c + 1) * 128], ident[:], kx + ["ident"], [kb])
                    act(X[:, half * 4:half * 4 + 4, ch * 128:(ch + 1) * 128], b_[:].rearrange("p (c t) -> p c t", c=4), AF.Copy, [kb],
                        [("X", half * 4 + c4, (ch // 4) * 512) for c4 in range(4)])
            if ps_ == 1:
                dma(XS[0][0:32, :], xs, [], KXS[0])
                b_, kb = psget("mm")
                for c in range(8):
                    tr(b_[:, c * 32:(c + 1) * 32], XS[0][0:32, c * 128:(c + 1) * 128], ident[0:32, 0:32], KXS[0] + ["ident"], [kb])
                act(X[:, :, 1024:1056], b_[:, 0:256].rearrange("p (c t) -> p c t", c=8), AF.Copy, [kb], [("X", c, 1024) for c in range(8)])
            llist = list(cfg["layer_list"] if cfg["layer_list"] is not None else range(NL))
            normed = False
            for li, l in enumerate(llist):
                if cfg["mixer"]:
                    if not normed:
                        pre_norm(0, l, tiles)
                    normed = False
                    nxt = (lambda t0, n, part=None, l=l: pre_norm_tile(2, l, t0, n, part)) if cfg["ffn"] else None
                    if l % 2 == 0:
                        even_mixer(ps_, l, tiles)
                        dense_out_residual(WO_v, kpg(WO_PG, 8), 8, lambda k, t0, n: YE_v[:, k, t0:t0 + n],
                                           lambda k, t0: kpg(27 + k), 1, l, tiles, after_tile=nxt)
                    else:
                        odd_mixer(ps_, l, tiles)
                        dense_out_residual(WO_v, kpg(WO_PG, 8), 8, lambda k, t0, n: HY[:, k, t0:t0 + n],
                                           lambda k, t0: [("HY", k, t0)], 1, l, tiles, after_tile=nxt)
                    normed = nxt is not None
                if cfg["ffn"]:
                    if not normed:
                        pre_norm(2, l, tiles)
                    normed = False
                    wd_th = resident_thunks(wd[l], KFF, WD_PG)
                    def gu_load(j):
                        sg_ = load_stage(wg[l], j * 256)
                        su_ = load_stage(wu[l], j * 256)
                        wd_th.pop(0)()
                        return (sg_, su_)

                    def gu_body(j, stg, t0, n):
                        sg_, su_ = stg
                        for jj in range(2):
                            blk = 2 * j + jj
                            bg_, kbg = fm_block(sg_, jj, t0, n)
                            bu_, kbu = fm_block(su_, jj, t0, n)
                            sgi = blk % 2
                            act(SG[sgi][:, 0:n], bg_[:, 0:n], AF.Silu, [kbg], [("SG", sgi)])
                            tt(A_v[:, blk, t0:t0 + n], bu_[:, 0:n], SG[sgi][:, 0:n], ALU.mult, [kbu, ("SG", sgi)], [("AR", blk)])
                    two_stage_split([lambda j=j: gu_load(j) for j in range(2)], gu_body, tiles)
                    for j in range(2, 11):
                        stg = gu_load(j)
                        for (t0, n) in tiles:
                            gu_body(j, stg, t0, n)
                    nxt = None
                    if li + 1 < len(llist) and cfg["mixer"]:
                        nxt = (lambda t0, n, part=None, l2=llist[li + 1]: pre_norm_tile(0, l2, t0, n, part))
                    dense_out_residual(WD_v, kpg(WD_PG, 22), KFF, lambda k, t0, n: A_v[:, k, t0:t0 + n],
                                       lambda k, t0: [("AR", k)], 3, l, tiles, after_tile=nxt)
                    normed = nxt is not None
            for ch in range(8):
                xs_ = XS[ch % 2]
                kx = KXS[ch % 2]
                for half in range(2):
                    b_, kb = psget("mm")
                    for c4 in range(4):
                        c = half * 4 + c4
                        tr(b_[:, c4 * 128:(c4 + 1) * 128], X[:, c, ch * 128:(ch + 1) * 128], ident[:], [("X", c, (ch // 4) * 512), "ident"], [kb])
                    act(xs_[:, half * 512:(half + 1) * 512], b_[:], AF.Copy, [kb], kx)
                dma(y_p[ps_ * NP + ch * 128:ps_ * NP + (ch + 1) * 128, :], xs_, kx, [("y_p", ps_, ch)])
                OUT_KEYS.append(("y_p", ps_, ch))
            if ps_ == 1:
                for half in range(2):
                    b_, kb = psget("mm")
                    for c4 in range(4):
                        c = half * 4 + c4
                        tr(b_[0:32, c4 * 128:(c4 + 1) * 128], X[:, c, 1024:1056], ident[:], [("X", c, 1024), "ident"], [kb])
                    act(XS[0][0:32, half * 512:(half + 1) * 512], b_[0:32, :], AF.Copy, [kb], KXS[0])
                dma(y_s, XS[0][0:32, :], KXS[0], ["y_s"])
                OUT_KEYS.append("y_s")
        P.add("sp", lambda e: None, reads=list(OUT_KEYS))
        with nc.allow_non_contiguous_dma(reason="small constant / broadcast loads"):
            P.emit(st)
        nc._prog_stats = (len(P.ops), dict(P.sig_counts))
    return nc


_CONSTS = None


def _consts():
    global _CONSTS
    if _CONSTS is None:
        ident = np.eye(128, dtype=np.float32)
        jmat = np.ascontiguousarray(ident[::-1])
        tri = np.triu(np.ones((128, 128), dtype=np.float32))
        rc = np.zeros((128, 4, 16), dtype=np.float32)
        for g, win in enumerate((2, 4, 8, 16)):
            rc[:, g, :] = 1.0 / np.minimum(np.arange(16) + 1, win)
        _CONSTS = dict(c_ident=ident, c_jmat=jmat, c_tri=tri, c_rcnt=rc.reshape(128, 64))
    return _CONSTS


def make_in_maps(inp):
    f = lambda a: np.ascontiguousarray(np.asarray(a, dtype=np.float32))
    norms = f(np.stack([inp["norm_mix_pre"], inp["norm_mix_post"], inp["norm_ffn_pre"], inp["norm_ffn_post"]]))
    shared = dict(
        norms=norms, wg=f(inp["ffn_w_gate"]), wu=f(inp["ffn_w_up"]), wd=f(inp["ffn_w_down"]),
        evin=f(inp["ev_w_in"]), evout=f(inp["ev_w_out"]), avg=f(inp["a_v_gain"]), asw=f(inp["a_spatial_w"]),
        asb=f(np.asarray(inp["a_spatial_b"]).reshape(2, 512)), bcw=f(inp["b_conv_w"]), odin=f(inp["od_w_in"]), odout=f(inp["od_w_out"]),
        cgw=f(inp["c_group_w"]), csc=f(inp["c_scale"]), drb=f(inp["d_rel_bias"]), **_consts())
    xp = f(inp["x_prompt"])
    xs = f(inp["x_sample"])
    cc = f(inp["cache_conv"])
    cpo = f(inp["cache_pool"])
    ck = f(inp["cache_k"]).reshape(2, 16, 512, 512)
    cv = f(inp["cache_v"]).reshape(2, 16, 512, 512)
    maps = []
    for i in range(NCORES):
        m = dict(shared)
        m["xp"] = xp[i]
        m["xs"] = np.ascontiguousarray(xs[2 * i:2 * i + 2].reshape(32, D))
        m["cconv"] = np.ascontiguousarray(cc[:, 2 * i:2 * i + 2])
        m["cpool"] = np.ascontiguousarray(cpo[:, 2 * i:2 * i + 2])
        m["ck"] = np.ascontiguousarray(ck[:, 2 * i:2 * i + 2])
        m["cv"] = np.ascontiguousarray(cv[:, 2 * i:2 * i + 2])
        maps.append(m)
    return maps


def assemble(res):
    R = res
    y_p = np.stack([r["y_p"] for r in R])
    y_s = np.concatenate([r["y_s"].reshape(2, 16, D) for r in R], axis=0)
    conv_p = np.stack([r["o_conv_p"] for r in R], axis=1)
    pool_p = np.stack([r["o_pool_p"] for r in R], axis=1)
    k_p = np.stack([r["o_k_p"].reshape(2, 512, 8, 64) for r in R], axis=1)
    v_p = np.stack([r["o_v_p"].reshape(2, 512, 8, 64) for r in R], axis=1)
    av_s = np.concatenate([r["o_av_s"].reshape(2, 2, 16, 512) for r in R], axis=1)
    conv_s = np.concatenate([r["o_conv_s"] for r in R], axis=1)
    pool_s = np.concatenate([r["o_pool_s"] for r in R], axis=1)
    k_s = np.concatenate([r["o_k_s"].reshape(2, 2, 16, 8, 64) for r in R], axis=1)
    v_s = np.concatenate([r["o_v_s"].reshape(2, 2, 16, 8, 64) for r in R], axis=1)
    outs = (y_p, y_s, conv_p, pool_p, k_p, v_p, av_s, conv_s, pool_s, k_s, v_s)
    return tuple(np.ascontiguousarray(o, dtype=np.float32) for o in outs)


_NC_CACHE = {}


def kernel(**inputs):
    key = "full"
    if key not in _NC_CACHE:
        _NC_CACHE[key] = build_program()
    nc = _NC_CACHE[key]
    in_maps = make_in_maps(inputs)
    res = run_bass_kernel_spmd(nc, in_maps, core_ids=list(range(NCORES)))
    return assemble(res.results)
```

```python
import os
import numpy as np
import concourse.bass as bass
import concourse.mybir as mybir
from concourse.bass_utils import run_bass_kernel_spmd
from contextlib import ExitStack

F32 = mybir.dt.float32
BF16 = mybir.dt.bfloat16
AF = mybir.ActivationFunctionType
ALU = mybir.AluOpType

ENGS = ("pe", "act", "dve", "pool", "sp")


class _Op:
    __slots__ = ("eng", "fn", "deps", "signal", "dma", "sig")

    def __init__(self, eng, fn, deps, dma):
        self.eng = eng
        self.fn = fn
        self.deps = deps
        self.dma = dma
        self.signal = False
        self.sig = None


class Prog:
    def __init__(self, nc, n_dma_sems=16, same_engine_sync=True):
        self.nc = nc
        self.ops = []
        self.lastw = {}
        self.readers = {}
        self.n_dma_sems = n_dma_sems
        self.same_engine_sync = same_engine_sync

    def add(self, eng, fn, reads=(), writes=(), dma=False):
        idx = len(self.ops)
        deps = set()
        for k in reads:
            w = self.lastw.get(k)
            if w is not None:
                deps.add(w)
        for k in writes:
            w = self.lastw.get(k)
            if w is not None:
                deps.add(w)
            for r in self.readers.get(k, ()):
                deps.add(r)
        for k in reads:
            self.readers.setdefault(k, []).append(idx)
        for k in writes:
            self.lastw[k] = idx
            self.readers[k] = []
        deps.discard(idx)
        self.ops.append(_Op(eng, fn, deps, dma))
        return idx

    def dma(self, q, out, in_, reads=(), writes=(), **kw):
        return self.add(q, lambda e: e.dma_start(out=out, in_=in_, **kw), reads, writes, dma=True)

    def emit(self, stack):
        nc = self.nc
        ops = self.ops
        for i, op in enumerate(ops):
            keep = set()
            for d in op.deps:
                p = ops[d]
                if p.eng == op.eng and not p.dma and not op.dma:
                    if op.eng == "pe" or not self.same_engine_sync:
                        continue
                keep.add(d)
            newest = {}
            pruned = set()
            for d in keep:
                p = ops[d]
                if p.dma:
                    pruned.add(d)
                elif p.eng not in newest or newest[p.eng] < d:
                    newest[p.eng] = d
            pruned.update(newest.values())
            op.deps = pruned
            for d in pruned:
                ops[d].signal = True
        sems = {e: stack.enter_context(nc.semaphore("s_" + e)) for e in ENGS if e != "sp"}
        dsems = {}
        for q in ("sp", "pool", "act"):
            if any(o.dma and o.eng == q for o in ops):
                dsems[q] = [stack.enter_context(nc.semaphore("d_%s%d" % (q, j))) for j in range(self.n_dma_sems)]
        cnt = {e: 0 for e in ENGS}
        dcnt = {q: 0 for q in dsems}
        prewait = {}
        for i, op in enumerate(ops):
            if op.dma:
                n = dcnt[op.eng]
                dcnt[op.eng] += 1
                slot = n % self.n_dma_sems
                rnd = n // self.n_dma_sems
                op.sig = (dsems[op.eng][slot], 16 * (rnd + 1))
                if rnd > 0:
                    prewait[i] = (dsems[op.eng][slot], 16 * rnd)
            elif op.signal:
                cnt[op.eng] += 1
                op.sig = (sems[op.eng], cnt[op.eng])
        self.sig_counts = dict(cnt)
        per_eng = {e: [] for e in ENGS}
        for i, op in enumerate(ops):
            per_eng[op.eng].append(i)

        def run_engine(e, eng):
            waited = {}
            for i in per_eng[e]:
                op = ops[i]
                ws = [ops[d].sig for d in op.deps]
                if i in prewait:
                    ws.append(prewait[i])
                best = {}
                for (s, v) in ws:
                    key = id(s)
                    if key not in best or best[key][1] < v:
                        best[key] = (s, v)
                for key, (s, v) in best.items():
                    if waited.get(key, 0) < v:
                        eng.wait_ge(s, v)
                        waited[key] = v
                inst = op.fn(eng)
                if inst is None:
                    continue
                if op.dma:
                    inst.then_inc(op.sig[0], 16)
                elif op.signal:
                    inst.then_inc(op.sig[0], 1)

        with nc.Block() as block:
            if per_eng["sp"]:
                @block.sync
                def _(eng):
                    run_engine("sp", eng)
            if per_eng["pe"]:
                @block.tensor
                def _(eng):
                    run_engine("pe", eng)
            if per_eng["act"]:
                @block.scalar
                def _(eng):
                    run_engine("act", eng)
            if per_eng["dve"]:
                @block.vector
                def _(eng):
                    run_engine("dve", eng)
            if per_eng["pool"]:
                @block.gpsimd
                def _(eng):
                    run_engine("pool", eng)


NCORES = 8
D = 1024
SEQ = 2048
NP = 1024
TOK = 1056
FFN = 2816
KFF = 22
EPS = 1e-6
PG = 1056
NPAGES = 44
WO_PG = 36
WD_PG = 22

CFG = {"attn_biasmask": 3, "attn_hps": 4, "attn_nobias": False, "attn_chunks": None, "attn_parts": 4, "odd_attn": True, "odd_pool": True, "odd_sattn": True, "layers": 4, "layer_list": None, "mixer": True, "ffn": True, "same_engine_sync": True}


def build_program(cfg=None):
    cfg = dict(CFG, **(cfg or {}))
    NL = cfg["layers"]
    nc = bass.Bass("TRN2", target_bir_lowering=False)

    def din(name, shape):
        return nc.dram_tensor(name, list(shape), F32, kind="ExternalInput").ap()

    def dout(name, shape):
        return nc.dram_tensor(name, list(shape), F32, kind="ExternalOutput").ap()

    xp = din("xp", [SEQ, D])
    xs = din("xs", [32, D])
    cconv = din("cconv", [2, 2, 2, 512])
    cpool = din("cpool", [2, 2, 15, 512])
    ck = din("ck", [2, 2, 512, 512])
    cv = din("cv", [2, 2, 512, 512])
    norms = din("norms", [4, 4, D])
    wg = din("wg", [4, D, FFN])
    wu = din("wu", [4, D, FFN])
    wd = din("wd", [4, FFN, D])
    evin = din("evin", [2, D, 2560])
    evout = din("evout", [2, D, D])
    avg = din("avg", [2, 512])
    asw = din("asw", [2, 4, 128, 128])
    asb = din("asb", [2, 512])
    bcw = din("bcw", [2, 3, 512])
    odin = din("odin", [2, D, 2048])
    odout = din("odout", [2, D, D])
    cgw = din("cgw", [2, 4, 128, 128])
    csc = din("csc", [2, 512])
    drb = din("drb", [2, 8, 129])
    c_ident = din("c_ident", [128, 128])
    c_jmat = din("c_jmat", [128, 128])
    c_tri = din("c_tri", [128, 128])
    c_rcnt = din("c_rcnt", [128, 64])

    y_p = dout("y_p", [SEQ, D])
    y_s = dout("y_s", [32, D])
    o_conv_p = dout("o_conv_p", [2, 2, 512])
    o_pool_p = dout("o_pool_p", [2, 15, 512])
    o_k_p = dout("o_k_p", [2, 512, 512])
    o_v_p = dout("o_v_p", [2, 512, 512])
    o_av_s = dout("o_av_s", [2, 32, 512])
    o_conv_s = dout("o_conv_s", [2, 2, 2, 512])
    o_pool_s = dout("o_pool_s", [2, 2, 15, 512])
    o_k_s = dout("o_k_s", [2, 32, 512])
    o_v_s = dout("o_v_s", [2, 32, 512])
    OUT_KEYS = []

    kt_stash = [nc.dram_tensor("kt_stash%d" % o, [128, 4 * 512], BF16, kind="Internal").ap() for o in range(2)]
    v_stash = [nc.dram_tensor("v_stash%d" % o, [128, 4 * 520], BF16, kind="Internal").ap() for o in range(2)]
    rb_ext = [nc.dram_tensor("rb_ext%d" % o, [8, 256], F32, kind="Internal").ap() for o in range(2)]

    with ExitStack() as st:
        def sb(name, shape, dt=F32):
            return st.enter_context(nc.sbuf_tensor(name, list(shape), dt))

        P = Prog(nc, same_engine_sync=cfg["same_engine_sync"])

        def mm(out, lhsT, rhs, start, stop, r, w):
            P.add("pe", lambda e: e.matmul(out, lhsT=lhsT, rhs=rhs, start=start, stop=stop), r, w)

        def tr(out, in_, ident_, r, w):
            P.add("pe", lambda e: e.transpose(out=out, in_=in_, identity=ident_), r, w)

        def act(out, in_, func, r, w, **kw):
            P.add("act", lambda e: e.activation(out=out, in_=in_, func=func, **kw), r, w)

        def tt(out, in0, in1, op, r, w, eng="dve"):
            P.add(eng, lambda e: e.tensor_tensor(out=out, in0=in0, in1=in1, op=op), r, w)

        def ts(out, in0, s1, s2, op0, op1, r, w, eng="dve"):
            if op1 is None:
                P.add(eng, lambda e: e.tensor_scalar(out=out, in0=in0, scalar1=s1, scalar2=None, op0=op0), r, w)
            else:
                P.add(eng, lambda e: e.tensor_scalar(out=out, in0=in0, scalar1=s1, scalar2=s2, op0=op0, op1=op1), r, w)

        def stt(out, in0, scalar, in1, op0, op1, r, w):
            P.add("dve", lambda e: e.scalar_tensor_tensor(out=out, in0=in0, scalar=scalar, in1=in1, op0=op0, op1=op1), r, w)

        def cp(out, in_, r, w, eng="dve"):
            P.add(eng, lambda e: e.tensor_copy(out=out, in_=in_), r, w)

        def mset(ap, val, w, eng="dve"):
            P.add(eng, lambda e: e.memset(ap, val), (), w)

        def dma(out, in_, r, w, q="sp"):
            P.dma(q, out, in_, r, w)

        X = sb("X", [128, 8, TOK])
        HY = sb("HY", [128, 8, TOK], BF16)
        FB = sb("FB", [128, 8, 512])
        AR = sb("AR", [128, NPAGES * PG], BF16)
        NWST = 5
        WST = [sb("WST%d" % i, [128, 8, 256], BF16) for i in range(NWST)]
        SQ = [sb("SQ%d" % i, [128, 512], BF16) for i in range(4)]
        SG = [sb("SG%d" % i, [128, 512]) for i in range(2)]
        FBf = FB[:].rearrange("p c t -> p (c t)")
        XS = [FBf[:, 0:1024], FBf[:, 1024:2048]]
        KXS = [[("FB", 0), ("FB", 1)], [("FB", 2), ("FB", 3)]]
        SMALL = FBf[:, 3072:4096]
        KSM = [("FB", 6), ("FB", 7)]
        ident = sb("ident", [128, 128])
        identb = sb("identb", [128, 128], BF16)
        jb = sb("jb", [128, 128], BF16)
        onesb = sb("onesb", [128, 128], BF16)
        rcnt = sb("rcnt", [128, 4, 16])
        epsc = sb("epsc", [128, 1])
        GC = sb("GC", [128, 128])
        VC = sb("VC", [128, 32])
        VGB = sb("VGB", [128, 512])
        WST_T = [sb("WsT%d" % e, [128, 4, 128], BF16) for e in range(2)]
        BSH = sb("BSH", [1, 512], BF16)
        BSL = sb("BSL", [1, 512], BF16)
        WGB = [sb("WGB%d" % o, [128, 4, 128], BF16) for o in range(2)]
        TH65 = [sb("TH65_%d" % i, [128, 8, 64], BF16) for i in range(2)]
        TH1 = [sb("TH1_%d" % i, [128, 8, 64], BF16) for i in range(2)]
        THB = [sb("THB_%d" % i, [128, 8, 64], BF16) for i in range(2)]
        THC = [sb("THC_%d" % i, [128, 8, 16], BF16) for i in range(2)]
        CFAR = sb("CFAR", [128, 8])
        ZH = sb("ZH", [128, 2, 4, 2])
        PH = sb("PH", [128, 2, 4, 15])
        ZF = sb("ZF", [128, 4, 3, 2])
        ZSb = sb("ZSb", [128, 4, 2, 18], BF16)
        PS = sb("PS", [128, 4, 2, 32])
        PST = [sb("PST%d" % i, [128, 2, 32]) for i in range(2)]
        BN6 = [sb("BN6_%d" % i, [128, 6]) for i in range(3)]
        MV = [sb("MV%d" % i, [128, 2]) for i in range(3)]
        RS = [sb("RS%d" % i, [128, 2]) for i in range(3)]
        REC = [sb("REC%d" % i, [64, 8]) for i in range(2)]
        VSN = sb("VSN", [16, 2, 8, 65], BF16)
        KSN = sb("KSN", [128, 4, 32], BF16)
        ps = [st.enter_context(nc.psum_tensor("ps%d" % i, [128, 512], F32)) for i in range(8)]

        pools = {"mm": [0, 1, 2, 3], "aux": [4, 5], "st": [6, 7]}
        pool_ctr = {k: 0 for k in pools}

        def psget(pool):
            lst = pools[pool]
            i = lst[pool_ctr[pool] % len(lst)]
            pool_ctr[pool] += 1
            return ps[i], ("ps", i)

        def pg(p0, n=1):
            return AR[:, p0 * PG:(p0 + n) * PG]

        def kpg(p0, n=1):
            return [("AR", p) for p in range(p0, p0 + n)]

        A_v = pg(0, 22).rearrange("p (k t) -> p k t", k=22)
        WD_v = pg(WD_PG, 22).rearrange("p (k t) -> p k t", k=22)
        WO_v = pg(WO_PG, 8).rearrange("p (k t) -> p k t", k=8)
        YE_v = pg(27, 8).rearrange("p (k t) -> p k t", k=8)

        def tiles_of(ps_):
            t = [(0, 512), (512, 512)]
            if ps_ == 1:
                t.append((1024, 32))
            return t

        dma(ident[:], c_ident, [], ["ident"])
        dma(rcnt[:].rearrange("p g n -> p (g n)"), c_rcnt, [], ["rcnt"])
        cp(identb[:], ident[:], ["ident"], ["identb"])
        jf = XS[1][:, 0:128]
        dma(jf, c_jmat, [], KXS[1])
        cp(jb[:], jf, KXS[1], ["jb"])
        mset(onesb[:], 1.0, ["onesb"])
        mset(epsc[:], EPS, ["epsc"])
        GROW = XS[1][:, 128:256]
        VROW = XS[1][0:32, 256:384]
        tri = XS[1][:, 384:512]
        dma(GROW, norms.rearrange("t l (c p) -> (t l c) p", p=128), [], KXS[1])
        dma(VROW[0:24, :], bcw.rearrange("e k (c p) -> (e k c) p", p=128), [], KXS[1])
        dma(VROW[24:32, :], csc.rearrange("o (g p) -> (o g) p", p=128), [], KXS[1])
        dma(tri, c_tri, [], KXS[1])
        b_, kb = psget("aux")
        tr(b_[:, 0:128], GROW, ident[:], KXS[1] + ["ident"], [kb])
        act(GC[:], b_[:, 0:128], AF.Copy, [kb], ["GC"])
        b_, kb = psget("aux")
        tr(b_[:, 0:32], VROW, ident[0:32, 0:32], KXS[1] + ["ident"], [kb])
        act(VC[:], b_[:, 0:32], AF.Copy, [kb], ["VC"])

        def gcol(t, l, c):
            i = (t * 4 + l) * 8 + c
            return GC[:, i:i + 1]

        def convw(e, k, c):
            i = (e * 3 + k) * 4 + c
            return VC[:, i:i + 1]

        def cscale(o, g):
            i = 24 + o * 4 + g
            return VC[:, i:i + 1]

        for e in range(2):
            dma(XS[0][:, 0:512].rearrange("p (g j) -> p g j", g=4), asw[e].rearrange("g i j -> i g j"), [], KXS[0])
            b_, kb = psget("aux")
            for g in range(4):
                tr(b_[:, g * 128:(g + 1) * 128], XS[0][:, g * 128:(g + 1) * 128], ident[:], KXS[0] + ["ident"], [kb])
            tt(WST_T[e][:], b_[:].rearrange("p (g i) -> p g i", g=4), tri[:, None, :].to_broadcast([128, 4, 128]), ALU.mult,
               [kb] + KXS[1], [("WsT", e)])
        for o in range(2):
            dma(WGB[o][:], cgw[o].rearrange("g c d -> c g d"), [], [("WGB", o)], q="pool")
            RBS = XS[1][0:8, 512:768]
            dma(RBS[:, 0:129], drb[o], [], KXS[1])
            cp(RBS[:, 129:256], RBS[:, 128:129].to_broadcast([8, 127]), KXS[1], KXS[1])
            dma(rb_ext[o], RBS, KXS[1], [("rbx", o)])
        odd_built = [False]

        def build_even_consts(e):
            dma(VGB[:], bass.AP(avg.tensor, e * 512, [[0, 128], [1, 512]]), [], ["VGB"])
            dma(SMALL[0:1, 0:512], asb[e:e + 1, :], [], KSM)
            cp(BSH[:], SMALL[0:1, 0:512], KSM, ["BSH"])
            cp(SMALL[0:1, 512:1024], BSH[:], ["BSH"], KSM)
            tt(BSL[:], SMALL[0:1, 0:512], SMALL[0:1, 512:1024], ALU.subtract, KSM, ["BSL"])

        def build_odd_consts(o):
            for h in range(8):
                dma(CFAR[:, h:h + 1], bass.AP(drb.tensor, (o * 8 + h) * 129 + 128, [[0, 128], [1, 1]]), [], ["CFAR"])
            for (TH, kh, kl, off, r0, cols) in ((TH65[o], ("TH65h", o), "TH65l", 65, 0, 64), (TH1[o], ("TH1h", o), "TH1l", 1, 0, 64),
                                                (THB[o], ("THBh", o), "THBl", 1, 64, 64), (THC[o], ("THCh", o), "THCl", 49, 112, 16)):
                rows = 128
                src = bass.AP(rb_ext[o].tensor, off, [[1, 128 - r0], [256, 8], [1, cols]])
                tmp = SMALL[0:rows, 0:8 * cols].rearrange("p (h q) -> p h q", h=8)
                tmp2 = SMALL[0:rows, 512:512 + 8 * cols].rearrange("p (h q) -> p h q", h=8)
                if r0 > 0:
                    mset(tmp, 0.0, KSM)
                dma(tmp[r0:128], src, [("rbx", o)], KSM)
                tt(tmp, tmp, CFAR[0:rows, :, None].to_broadcast([rows, 8, cols]), ALU.subtract, KSM + ["CFAR"], KSM)
                ts(TH[:], tmp, 8.0, None, ALU.mult, None, KSM, [kh])

        wst_ctr = [0]

        def load_stage(Wsrc, col0):
            i = wst_ctr[0] % NWST
            wst_ctr[0] += 1
            src = Wsrc[:, col0:col0 + 256].rearrange("(c p) n -> p c n", p=128)
            dma(WST[i][:], src, [], [("WST", i)], q="pool")
            return i

        def stats_to_rstd(sbank, skey, n):
            act(sbank[:, 0:n], sbank[:, 0:n], AF.Ln, [skey, "epsc"], [skey], scale=1.0 / D, bias=epsc[:])
            act(sbank[:, 0:n], sbank[:, 0:n], AF.Exp, [skey], [skey], scale=-0.5)

        sq_ctr = [0]

        def sq_get():
            i = sq_ctr[0] % 4
            sq_ctr[0] += 1
            return SQ[i], ("SQ", i)

        def pre_norm(ntype, l, tiles):
            for (t0, n) in tiles:
                pre_norm_tile(ntype, l, t0, n)

        def pre_norm_tile(ntype, l, t0, n, part=None):
            if part in (None, "A"):
                if n <= 64:
                    act(HY[:, :, t0:t0 + n], X[:, :, t0:t0 + n], AF.Square, [("X", c, t0) for c in range(8)], [("HY", c, t0) for c in range(8)])
                else:
                    for c in range(8):
                        act(HY[:, c, t0:t0 + n], X[:, c, t0:t0 + n], AF.Square, [("X", c, t0)], [("HY", c, t0)])
            if part in (None, "B"):
                sbank, skey = psget("st")
                for c in range(8):
                    mm(sbank[:, 0:n], onesb[:], HY[:, c, t0:t0 + n], c == 0, c == 7, [("HY", c, t0), "onesb"], [skey])
                stats_to_rstd(sbank, skey, n)
                for c in range(8):
                    stt(HY[:, c, t0:t0 + n], X[:, c, t0:t0 + n], gcol(ntype, l, c), sbank[:, 0:n], ALU.mult, ALU.mult,
                        [("X", c, t0), skey, "GC"], [("HY", c, t0)])

        def dense_out_residual(Wv, wkeys, KC, src_fn, src_keys_fn, ntype, l, tiles, after_tile=None):
            run_deferred()
            prev_tile = None
            for (t0, n) in tiles:
                sbank, skey = psget("st")
                pendq = []
                if n <= 64:
                    b_, kb = psget("mm")
                    for nb in range(8):
                        for k in range(KC):
                            mm(b_[:, nb * n:(nb + 1) * n], Wv[:, k, nb * 128:(nb + 1) * 128], src_fn(k, t0, n), k == 0, k == KC - 1,
                               wkeys + src_keys_fn(k, t0), [kb])
                    if after_tile is not None and prev_tile is not None:
                        after_tile(*prev_tile)
                        prev_tile = None
                    s_, sk = sq_get()
                    act(s_[:, 0:8 * n], b_[:, 0:8 * n], AF.Square, [kb], [sk])
                    i0 = (ntype * 4 + l) * 8
                    fbv = FB[:, :, 0:n]
                    kfb = [("FB", nb) for nb in range(8)]
                    tt(fbv, b_[:, 0:8 * n].rearrange("p (c t) -> p c t", c=8), GC[:, i0:i0 + 8, None].to_broadcast([128, 8, n]), ALU.mult,
                       [kb, "GC"], kfb)
                    for nb in range(8):
                        mm(sbank[:, 0:n], onesb[:], s_[:, nb * n:(nb + 1) * n], nb == 0, nb == 7, [sk, "onesb"], [skey])
                    stats_to_rstd(sbank, skey, n)
                    tt(fbv, fbv, sbank[:, None, 0:n].to_broadcast([128, 8, n]), ALU.mult, kfb + [skey], kfb)
                    xk = [("X", nb, t0) for nb in range(8)]
                    tt(X[:, :, t0:t0 + n], X[:, :, t0:t0 + n], fbv, ALU.add, kfb + xk, xk)
                    prev_tile = (t0, n)
                    continue
                for nb in range(8):
                    b_, kb = psget("mm")
                    for k in range(KC):
                        mm(b_[:, 0:n], Wv[:, k, nb * 128:(nb + 1) * 128], src_fn(k, t0, n), k == 0, k == KC - 1,
                           wkeys + src_keys_fn(k, t0), [kb])
                    if len(pendq) > 1:
                        pendq.pop(0)()
                    if after_tile is not None and prev_tile is not None and nb == (4 if KC > 8 else 5):
                        after_tile(*prev_tile, part="A")
                    if after_tile is not None and prev_tile is not None and nb == (6 if KC > 8 else 7):
                        after_tile(*prev_tile, part="B")
                        prev_tile = None
                    s_, sk = sq_get()
                    act(s_[:, 0:n], b_[:, 0:n], AF.Square, [kb], [sk])
                    act(FB[:, nb, 0:n], b_[:, 0:n], AF.Copy, [kb, "GC"], [("FB", nb)], scale=gcol(ntype, l, nb))

                    def pend(s_=s_, sk=sk, nb=nb):
                        mm(sbank[:, 0:n], onesb[:], s_[:, 0:n], nb == 0, nb == 7, [sk, "onesb"], [skey])
                    pendq.append(pend)
                while pendq:
                    pendq.pop(0)()
                stats_to_rstd(sbank, skey, n)
                for nb in range(8):
                    tt(FB[:, nb, 0:n], FB[:, nb, 0:n], sbank[:, 0:n], ALU.mult, [("FB", nb), skey], [("FB", nb)])
                    tt(X[:, nb, t0:t0 + n], X[:, nb, t0:t0 + n], FB[:, nb, 0:n], ALU.add, [("FB", nb), ("X", nb, t0)], [("X", nb, t0)])
                if after_tile is not None and prev_tile is not None:
                    after_tile(*prev_tile)
                prev_tile = (t0, n)
            if after_tile is not None and prev_tile is not None:
                deferred.append(lambda pt=prev_tile: after_tile(*pt))

        deferred = []

        def run_deferred():
            while deferred:
                deferred.pop(0)()

        def two_stage_split(loaders, body, tiles):
            stgs = [ld() for ld in loaders]
            for s_, stg in enumerate(stgs):
                for (t0, n) in tiles[:-1]:
                    body(s_, stg, t0, n)
            run_deferred()
            for s_, stg in enumerate(stgs):
                body(s_, stg, *tiles[-1])

        def resident_thunks(Wsrc, KC, page0):
            th = []
            for k in range(0, KC, 2):
                def f(k=k):
                    dst = pg(page0 + k, 2).rearrange("p (k t) -> p k t", k=2)[:, :, 0:1024]
                    src = Wsrc[k * 128:(k + 2) * 128, :].rearrange("(k p) n -> p k n", p=128)
                    dma(dst, src, [], kpg(page0 + k, 2), q="pool")
                th.append(f)
            return th

        def fm_block(stage, jj, t0, n):
            b_, kb = psget("mm")
            for k in range(8):
                mm(b_[:, 0:n], WST[stage][:, k, jj * 128:(jj + 1) * 128], HY[:, k, t0:t0 + n], k == 0, k == 7,
                   [("WST", stage), ("HY", k, t0)], [kb])
            return b_, kb

        def tm_block(stages, t0, m):
            b_, kb = psget("mm")
            for half, stg in enumerate(stages):
                for k in range(8):
                    mm(b_[0:m, half * 256:(half + 1) * 256], HY[:, k, t0:t0 + m], WST[stg][:, k, :], k == 0, k == 7,
                       [("WST", stg), ("HY", k, (t0 // 512) * 512)], [kb])
            return b_, kb

        def even_mixer(ps_, l, tiles):
            e = l // 2
            Wsrc = evin[e]
            XB = [pg(2 * c, 2).bitcast(F32) for c in range(4)]
            kXB = [kpg(2 * c, 2) for c in range(4)]
            Z = [pg(8 + 2 * c, 2).bitcast(F32) for c in range(4)]
            kZ = [kpg(8 + 2 * c, 2) for c in range(4)]
            T1 = [pg(17 + i).bitcast(F32) for i in range(2)]
            kT1 = [kpg(17 + i) for i in range(2)]
            SSB = pg(19, 4).rearrange("p (g t) -> p g t", g=4)
            kSSB = kpg(19, 4)
            VNt = [pg(p_)[:, 0:512] for p_ in (23, 16, 35)]
            kVNl = [kpg(p_) for p_ in (23, 16, 35)]
            LT = [pg(24 + i).bitcast(F32)[:, 0:512] for i in range(3)]
            kLT = [kpg(24 + i) for i in range(3)]
            wo_th = resident_thunks(evout[e], 8, WO_PG)
            def xb_body(s, stg, t0, n):
                for jj in range(2):
                    c = 2 * s + jj
                    b_, kb = fm_block(stg, jj, t0, n)
                    act(XB[c][:, t0:t0 + n], b_[:, 0:n], AF.Copy, [kb], kXB[c])
            two_stage_split([lambda s=s: load_stage(Wsrc, 2048 + s * 256) for s in range(2)], xb_body, tiles)
            build_even_consts(e)
            if not odd_built[0]:
                odd_built[0] = True
                for o_ in range(2):
                    build_odd_consts(o_)
            Zb = pg(8, 5)[:, 0:4 * 1058].rearrange("p (c t) -> p c t", c=4)
            kZb = kpg(8, 5)
            DG = pg(13, 2)[:, 0:1536].rearrange("p (i j) -> p i j", i=12)
            kDG = kpg(13, 2)
            for c in range(4):
                for k in range(3):
                    ts(DG[:, c * 3 + k, :], identb[:], convw(e, k, c), None, ALU.mult, None, ["identb", "VC"], kDG)
            if ps_ == 0:
                mset(Zb[:, :, 0:2], 0.0, kZb)
            else:
                cp(Zb[:, :, 0:2], ZH[:, e], [("ZH", e)], kZb)
            for s in range(2):
                stg = load_stage(Wsrc, 1536 + s * 256)
                if s == 1:
                    for f_ in wo_th:
                        f_()
                for (t0, n) in tiles:
                    for jj in range(2):
                        c = 2 * s + jj
                        b_, kb = fm_block(stg, jj, t0, n)
                        tt(Zb[:, c, 2 + t0:2 + t0 + n], b_[:, 0:n], XB[c][:, t0:t0 + n], ALU.mult, [kb] + kXB[c], kZb)
                        if t0 == 512:
                            tt(ZF[:, c, 0, :], b_[:, 510:512], XB[c][:, 1022:1024], ALU.mult, [kb] + kXB[c], [("ZF", c)])
                        if t0 == 1024:
                            tt(ZF[:, c, 1, :], b_[:, 14:16], XB[c][:, 1038:1040], ALU.mult, [kb] + kXB[c], [("ZF", c)])
                            tt(ZF[:, c, 2, :], b_[:, 30:32], XB[c][:, 1054:1056], ALU.mult, [kb] + kXB[c], [("ZF", c)])
            for c in range(4):
                for (t0, n) in tiles[:2]:
                    b_, kb = psget("mm")
                    for k in range(3):
                        mm(b_[:, 0:n], DG[:, c * 3 + k, :], Zb[:, c, t0 + k:t0 + k + n], k == 0, k == 2, kDG + kZb, [kb])
                    act(XB[c][:, t0:t0 + n], b_[:, 0:n], AF.Copy, [kb], kXB[c])
            if ps_ == 0:
                for c in range(4):
                    cp(ZH[:, e, c, :], ZF[:, c, 0, :], [("ZF", c)], [("ZH", e)])
            if ps_ == 1:
                b_, kb = psget("aux")
                for c in range(4):
                    tr(b_[0:2, c * 128:(c + 1) * 128], ZF[:, c, 0, :], ident[:], [("ZF", c), "ident"], [kb])
                act(SMALL[0:2, 0:512], b_[0:2, 0:512], AF.Copy, [kb], KSM)
                dma(o_conv_p[e], SMALL[0:2, 0:512], KSM, [("o_conv_p", e)])
                OUT_KEYS.append(("o_conv_p", e))
                dma(SMALL[0:4, 512:1024], cconv[e].rearrange("s k n -> (s k) n"), [], KSM)
                b_, kb = psget("aux")
                for c in range(4):
                    tr(b_[:, c * 4:(c + 1) * 4], SMALL[0:4, 512 + c * 128:512 + (c + 1) * 128], ident[0:4, 0:4], KSM + ["ident"], [kb])
                for c in range(4):
                    act(ZSb[:, c, :, 0:2], b_[:, c * 4:(c + 1) * 4].rearrange("p (s k) -> p s k", s=2), AF.Copy, [kb], [("ZSb", c)])
                    cp(ZSb[:, c, :, 2:18], Zb[:, c, 1026:1058].rearrange("p (s t) -> p s t", s=2), kZb, [("ZSb", c)])
                    b2, kb2 = psget("mm")
                    for k in range(3):
                        mm(b2[:, 0:32].rearrange("p (s t) -> p s t", s=2), DG[:, c * 3 + k, :], ZSb[:, c, :, k:k + 16], k == 0, k == 2,
                           kDG + [("ZSb", c)], [kb2])
                    act(XB[c][:, 1024:1056], b2[:, 0:32], AF.Copy, [kb2], kXB[c])
                for s in range(2):
                    b_, kb = psget("aux")
                    for c in range(4):
                        tr(b_[0:2, c * 128:(c + 1) * 128], ZF[:, c, 1 + s, :], ident[:], [("ZF", c), "ident"], [kb])
                    act(SMALL[0:2, 0:512], b_[0:2, 0:512], AF.Copy, [kb], KSM)
                    dma(o_conv_s[e, s], SMALL[0:2, 0:512], KSM, [("o_conv_s", e, s)])
                    OUT_KEYS.append(("o_conv_s", e, s))
            for s in range(2):
                stg = load_stage(Wsrc, 1024 + s * 256)
                for (t0, n) in tiles:
                    for jj in range(2):
                        c = 2 * s + jj
                        b_, kb = fm_block(stg, jj, t0, n)
                        tt(YE_v[:, 4 + c, t0:t0 + n], b_[:, 0:n], XB[c][:, t0:t0 + n], ALU.mult, [kb] + kXB[c], kpg(27 + 4 + c))
            stg_v = [load_stage(Wsrc, 512), load_stage(Wsrc, 768)]
            chunks = [(ch * 128, 128, None) for ch in range(8)]
            if ps_ == 1:
                chunks += [(1024, 16, 0), (1040, 16, 1)]
            gate_pend = []
            for ci, (t0, m, sidx) in enumerate(chunks):
                b_, kb = tm_block(stg_v, t0, m)
                i2 = ci % 3
                kVN = kVNl[i2]
                P.add("dve", lambda e_, b_=b_, m=m, i2=i2: e_.bn_stats(out=BN6[i2][0:m, :], in_=b_[0:m, :]), [kb], [("BN6", i2)])
                P.add("dve", lambda e_, m=m, i2=i2: e_.bn_aggr(out=MV[i2][0:m, :], in_=BN6[i2][0:m, :]), [("BN6", i2)], [("MV", i2)])
                act(RS[i2][0:m, 0:1], MV[i2][0:m, 1:2], AF.Ln, [("MV", i2), "epsc"], [("RS", i2)], scale=1.0, bias=epsc[0:m, :])
                act(RS[i2][0:m, 1:2], RS[i2][0:m, 0:1], AF.Exp, [("RS", i2)], [("RS", i2)], scale=-0.5)
                ts(LT[i2][0:m, :], b_[0:m, :], MV[i2][0:m, 0:1], RS[i2][0:m, 1:2], ALU.subtract, ALU.mult, [kb, ("MV", i2), ("RS", i2)], kLT[i2])
                if sidx is None:
                    tt(VNt[i2][0:m, :], LT[i2][0:m, :], VGB[0:m, :], ALU.mult, kLT[i2] + ["VGB"], kVN)
                else:
                    tt(LT[i2][0:m, :], LT[i2][0:m, :], VGB[0:m, :], ALU.mult, kLT[i2] + ["VGB"], kLT[i2])
                    cp(VNt[i2][0:m, :], LT[i2][0:m, :], kLT[i2], kVN)
                    dma(o_av_s[e, sidx * 16:(sidx + 1) * 16, :], LT[i2][0:m, :], kLT[i2], [("o_av_s", e, sidx)])
                    OUT_KEYS.append(("o_av_s", e, sidx))
                def gate(t0=t0, m=m, i2=i2, kVN=kVN):
                    sbk, skb = psget("aux")
                    for g in range(4):
                        mm(sbk[:, g * 128:g * 128 + m], VNt[i2][0:m, g * 128:(g + 1) * 128], WST_T[e][0:m, g, 0:m], True, False,
                           kVN + [("WsT", e)], [skb])
                        mm(sbk[:, g * 128:g * 128 + m], onesb[0:1, :], BSH[0:1, g * 128:g * 128 + m], False, False, ["onesb", "BSH"], [skb])
                        mm(sbk[:, g * 128:g * 128 + m], onesb[0:1, :], BSL[0:1, g * 128:g * 128 + m], False, True, ["onesb", "BSL"], [skb])
                    act(SSB[:, :, t0:t0 + m], sbk[:].rearrange("p (g i) -> p g i", g=4)[:, :, 0:m], AF.Copy, [skb], kSSB)
                gate_pend.append(gate)
                if len(gate_pend) > 2:
                    gate_pend.pop(0)()
            while gate_pend:
                gate_pend.pop(0)()
            for s in range(2):
                stg = load_stage(Wsrc, s * 256)
                for (t0, n) in tiles:
                    for jj in range(2):
                        c = 2 * s + jj
                        b_, kb = fm_block(stg, jj, t0, n)
                        tt(YE_v[:, c, t0:t0 + n], b_[:, 0:n], SSB[:, c, t0:t0 + n], ALU.mult, [kb] + kSSB, kpg(27 + c))

        def odd_mixer(ps_, l, tiles):
            o = l // 2
            if not odd_built[0]:
                odd_built[0] = True
                for o_ in range(2):
                    build_odd_consts(o_)
            Wsrc = odin[o]
            Pb = [pg(2 * g, 2).bitcast(F32) for g in range(4)]
            kP = [kpg(2 * g, 2) for g in range(4)]
            TA = pg(8).bitcast(F32)
            TB = pg(9).bitcast(F32)
            kTA, kTB = kpg(8), kpg(9)
            POOLED = pg(10, 2).rearrange("p (g t) -> p g t", g=4)[:, :, 0:512]
            kPOOLED = kpg(10, 2)
            QTZ = pg(13, 8).rearrange("p (h t) -> p h t", h=8)
            kQT = kpg(13, 8)
            KTw = pg(21, 6)[:, 0:4 * 1568].rearrange("p (h t) -> p h t", h=4)
            kKT = kpg(21, 6)
            Vw = pg(27, 6)[:, 0:12 * 520].rearrange("p (m h d) -> p m h d", m=12, h=8)
            kV = kpg(27, 6)
            PT = [pg(30 + i)[:, 0:640].rearrange("p (a s q) -> p a s q", a=2, s=5) for i in range(2)]
            kPT = [kpg(30 + i) for i in range(2)]
            YD = [pg(32)[0:64, i * 512:(i + 1) * 512] for i in range(2)]
            kYD = kpg(32)
            KTMs = [pg(p_).bitcast(F32)[:, 0:512] for p_ in (0, 1, 2, 3)]
            kKTM = [kpg(p_) for p_ in (0, 1, 2, 3)]
            KFv = FB[:, 0:4, :]
            VF = [FB[:, 4 + i, :] for i in range(2)]
            wo_th = resident_thunks(odout[o], 8, WO_PG)
            def k_body(s, stg, t0, n):
                if True:
                    for jj in range(2):
                        hp = 2 * s + jj
                        b_, kb = fm_block(stg, jj, t0, n)
                        if t0 < 1024:
                            act(KTw[:, hp, 512 + t0:512 + t0 + n], b_[:, 0:n], AF.Copy, [kb], kKT)
                            if ps_ == 1 and t0 == 512:
                                act(KFv[:, hp, :], b_[:, 0:n], AF.Copy, [kb], [("FB", hp)])
                        else:
                            act(KSN[:, hp, :], b_[:, 0:n], AF.Copy, [kb], [("KSN", hp)])
                            act(KFv[:, hp, 0:32], b_[:, 0:n], AF.Copy, [kb], [("FB", hp)])
                            b2, kb2 = psget("aux")
                            tr(b2[0:32, 0:128], KFv[:, hp, 0:32], ident[:], [("FB", hp), "ident"], [kb2])
                            act(SMALL[0:32, hp * 128:(hp + 1) * 128], b2[0:32, 0:128], AF.Copy, [kb2], KSM)
                    if ps_ == 1 and t0 == 512:
                        for jj in range(2):
                            hp = 2 * s + jj
                            for ch in range(4):
                                b2, kb2 = psget("aux")
                                tr(b2[:, 0:128], KFv[:, hp, ch * 128:(ch + 1) * 128], ident[:], [("FB", hp), "ident"], [kb2])
                                act(KTMs[ch][:, hp * 128:(hp + 1) * 128], b2[:, 0:128], AF.Copy, [kb2], kKTM[ch])
            two_stage_split([lambda s=s: load_stage(Wsrc, 1024 + s * 256) for s in range(2)], k_body, tiles)
            if ps_ == 1:
                for ch in range(4):
                    dma(o_k_p[o, ch * 128:(ch + 1) * 128, :], KTMs[ch], kKTM[ch], [("o_k_p", o, ch)])
                    OUT_KEYS.append(("o_k_p", o, ch))
                dma(o_k_s[o], SMALL[0:32, 0:512], KSM, [("o_k_s", o)])
                OUT_KEYS.append(("o_k_s", o))
            if ps_ == 0:
                mset(KTw[:, :, 0:512], 0.0, kKT)
                mset(Vw[:, 0:4], 0.0, kV)
                mset(PH[:, o], 0.0, [("PH", o)])
            else:
                dma(KTw[:, :, 0:512], kt_stash[o].rearrange("p (h t) -> p h t", h=4), [("kt_stash", o)], kKT)
                dma(Vw[:, 0:4], v_stash[o].rearrange("p (m h d) -> p m h d", m=4, h=8), [("v_stash", o)], kV)
            mset(Vw[:, 4:12, :, 64:65], 1.0, kV)
            QTZp = pg(13, 8).rearrange("p (h a t) -> p h a t", h=4, a=2)
            mset(QTZp[64:128, :, 0, :], 0.0, kQT)
            mset(QTZp[0:64, :, 1, :], 0.0, kQT)
            for s in range(2):
                stg = load_stage(Wsrc, s * 256)
                for (t0, n) in tiles:
                    for jj in range(2):
                        g = 2 * s + jj
                        b_, kb = fm_block(stg, jj, t0, n)
                        act(Pb[g][:, t0:t0 + n], b_[:, 0:n], AF.Copy, [kb], kP[g])

            PBs = [pg(10, 2).rearrange("p (g t) -> p g t", g=4)[:, :, 0:512], pg(33, 2).rearrange("p (g t) -> p g t", g=4)[:, :, 0:512]]
            kPBs = [kpg(10, 2), kpg(33, 2)]
            pool_th = []

            def pool_thunk(ti, t0, n, g):
                pb = PBs[ti]
                kpb = kPBs[ti]
                win = 2 ** (g + 1)
                src_lo = t0 - (win - 2)
                cur, kcur = TB, kTB
                oth, koth = TA, kTA
                j0 = 16 - (win - 2)
                ln = n + (win - 2)
                if t0 == 0:
                    cp(TA[:, 1:16], PH[:, o, g, :], [("PH", o)], kTA)
                    cp(TA[:, 16:16 + n], Pb[g][:, 0:n], kP[g], kTA)
                    tt(TB[:, j0:j0 + ln], TA[:, j0:j0 + ln], TA[:, j0 - 1:j0 - 1 + ln], ALU.add, kTA, kTB)
                else:
                    tt(TB[:, j0:j0 + ln], Pb[g][:, src_lo:src_lo + ln], Pb[g][:, src_lo - 1:src_lo - 1 + ln], ALU.add, kP[g], kTB)
                for k in range(1, g + 1):
                    rem = win - 2 ** (k + 1)
                    j0 = 16 - rem
                    ln = n + rem
                    tt(oth[:, j0:j0 + ln], cur[:, j0:j0 + ln], cur[:, j0 - 2 ** k:j0 - 2 ** k + ln], ALU.add, kcur, koth)
                    cur, kcur, oth, koth = oth, koth, cur, kcur
                stt(pb[:, g, 0:n], cur[:, 16:16 + n], 1.0 / win, Pb[g][:, t0:t0 + n], ALU.mult, ALU.subtract, kcur + kP[g], kpb)
                if ps_ == 0 and t0 == 0:
                    tt(oth[:, 0:16], cur[:, 16:32], rcnt[:, g, :], ALU.mult, kcur + ["rcnt"], koth)
                    tt(pb[:, g, 0:16], oth[:, 0:16], Pb[g][:, 0:16], ALU.subtract, koth + kP[g], kpb)
            if cfg["odd_pool"]:
                for ti, (t0, n) in enumerate(tiles[:2]):
                    for g in range(4):
                        pool_th.append(lambda ti=ti, t0=t0, n=n, g=g: pool_thunk(ti, t0, n, g))
            stg_v = [load_stage(Wsrc, 1536), load_stage(Wsrc, 1792)]
            for f_ in wo_th:
                f_()
            chunks = [(ch * 128, 128, None) for ch in range(8)]
            if ps_ == 1:
                chunks += [(1024, 16, 0), (1040, 16, 1)]
            for ci, (t0, m, sidx) in enumerate(chunks):
                b_, kb = tm_block(stg_v, t0, m)
                bv = b_[0:m, :].rearrange("p (h d) -> p h d", h=8)
                if sidx is None:
                    act(Vw[:, 4 + ci, :, 0:64], bv, AF.Copy, [kb], kV)
                    if ps_ == 1 and ci >= 4:
                        i2 = ci % 2
                        act(VF[i2][:, :], b_[:, :], AF.Copy, [kb], [("FB", 4 + i2)])
                        dma(o_v_p[o, (ci - 4) * 128:(ci - 3) * 128, :], VF[i2][:, :], [("FB", 4 + i2)], [("o_v_p", o, ci)])
                        OUT_KEYS.append(("o_v_p", o, ci))
                else:
                    act(VSN[:, sidx, :, 0:64], bv, AF.Copy, [kb], [("VSN", sidx)])
                    i2 = sidx
                    act(VF[i2][0:16, :], b_[0:16, :], AF.Copy, [kb], [("FB", 4 + i2)])
                    dma(o_v_s[o, sidx * 16:(sidx + 1) * 16, :], VF[i2][0:16, :], [("FB", 4 + i2)], [("o_v_s", o, sidx)])
                    OUT_KEYS.append(("o_v_s", o, sidx))
                if pool_th:
                    pool_th.pop(0)()
            for s in range(2):
                stg = load_stage(Wsrc, 512 + s * 256)
                for (t0, n) in tiles:
                    for jj in range(2):
                        hp = 2 * s + jj
                        b_, kb = fm_block(stg, jj, t0, n)
                        act(QTZ[0:64, 2 * hp, t0:t0 + n], b_[0:64, 0:n], AF.Copy, [kb], kQT)
                        act(QTZ[64:128, 2 * hp + 1, t0:t0 + n], b_[64:128, 0:n], AF.Copy, [kb], kQT)

            while pool_th:
                pool_th.pop(0)()
            if cfg["odd_pool"]:
                for ti, (t0, n) in enumerate(tiles[:2]):
                    for g in range(4):
                        b_, kb = psget("mm")
                        mm(b_[:, 0:n], WGB[o][:, g, :], PBs[ti][:, g, 0:n], True, True, [("WGB", o)] + kPBs[ti], [kb])
                        act(HY[:, g, t0:t0 + n], b_[:, 0:n], AF.Copy, [kb, "VC"], [("HY", g, t0)], scale=cscale(o, g))
            for g in range(4):
                cp(PH[:, o, g, :], Pb[g][:, 1009:1024], kP[g], [("PH", o)])
            if ps_ == 1:
                b_, kb = psget("aux")
                for g in range(4):
                    tr(b_[0:15, g * 128:(g + 1) * 128], Pb[g][:, 1009:1024], ident[:], kP[g] + ["ident"], [kb])
                act(SMALL[0:15, 0:512], b_[0:15, 0:512], AF.Copy, [kb], KSM)
                dma(o_pool_p[o], SMALL[0:15, 0:512], KSM, [("o_pool_p", o)])
                OUT_KEYS.append(("o_pool_p", o))
                Pall = pg(0, 8).bitcast(F32).rearrange("p (g t) -> p g t", g=4)
                for s in range(2):
                    dma(SMALL[0:15, 512:1024], cpool[o, s], [], KSM)
                    b_, kb = psget("aux")
                    for g in range(4):
                        tr(b_[:, g * 16:g * 16 + 16], SMALL[0:16, 512 + g * 128:512 + (g + 1) * 128], ident[0:16, 0:16], KSM + ["ident"], [kb])
                    act(PS[:, :, s, 1:16], b_[:, 0:64].rearrange("p (g t) -> p g t", g=4)[:, :, 0:15], AF.Copy, [kb], [("PS", s)])
                    cp(PS[:, :, s, 16:32], Pall[:, :, 1024 + 16 * s:1040 + 16 * s], kpg(0, 8), [("PS", s)])
                    b3, kb3 = psget("aux")
                    for g in range(4):
                        tr(b3[0:15, g * 128:(g + 1) * 128], PS[:, g, s, 17:32], ident[:], [("PS", s), "ident"], [kb3])
                    act(SMALL[0:15, 0:512], b3[0:15, 0:512], AF.Copy, [kb3], KSM)
                    dma(o_pool_s[o, s], SMALL[0:15, 0:512], KSM, [("o_pool_s", o, s)])
                    OUT_KEYS.append(("o_pool_s", o, s))
                pbs = PBs[0]
                kpbs = kPBs[0]
                for g in range(4):
                    win = 2 ** (g + 1)
                    cur = PS[:, g]
                    kcur = [("PS", 0), ("PS", 1)]
                    for k in range(0, g + 1):
                        rem = win - 2 ** (k + 1)
                        j0 = 16 - rem
                        ln = 16 + rem
                        dst = PST[k % 2]
                        kdst = [("PST", k % 2)]
                        tt(dst[:, :, j0:j0 + ln], cur[:, :, j0:j0 + ln], cur[:, :, j0 - 2 ** k:j0 - 2 ** k + ln], ALU.add, kcur, kdst)
                        cur, kcur = dst, kdst
                    stt(pbs[:, g, 0:32].rearrange("p (s t) -> p s t", s=2), cur[:, :, 16:32], 1.0 / win, PS[:, g, :, 16:32], ALU.mult, ALU.subtract,
                        kcur + [("PS", 0), ("PS", 1)], kpbs)
                for g in range(4):
                    b_, kb = psget("mm")
                    mm(b_[:, 0:32], WGB[o][:, g, :], pbs[:, g, 0:32], True, True, [("WGB", o)] + kpbs, [kb])
                    act(HY[:, g, 1024:1056], b_[:, 0:32], AF.Copy, [kb, "VC"], [("HY", g, 1024)], scale=cscale(o, g))

            PTX = [pg(33 + i)[:, 0:1024].rearrange("p (a b s q) -> p a b s q", a=2, b=2, s=4) for i in range(2)]
            kPTX = [kpg(33 + i) for i in range(2)]
            PTZe = [pg(12)[:, i * 256:(i + 1) * 256].rearrange("p (h q) -> p h q", h=4) for i in range(2)]
            PTZo = [pg(12)[:, 512 + i * 256:512 + (i + 1) * 256].rearrange("p (h q) -> p h q", h=4) for i in range(2)]
            PTZs = [pg(35)[:, 512 + i * 256:512 + (i + 1) * 256].rearrange("p (h q) -> p h q", h=4) for i in range(2)]
            kPTZe = [[("PTZe", i)] for i in range(2)]
            kPTZo = [[("PTZo", i)] for i in range(2)]
            kPTZs = [[("PTZs", i)] for i in range(2)]
            kYD2 = [[("YD2", i)] for i in range(2)]
            FINE = kPTZe[0] + kPTZe[1] + kPTZo[0] + kPTZo[1] + kPTZs[0] + kPTZs[1] + kYD2[0] + kYD2[1]
            YD2 = [pg(35)[:, i * 256:(i + 1) * 256] for i in range(2)]
            mset(pg(12), 0.0, kpg(12) + FINE)
            mset(pg(35), 0.0, kpg(35) + FINE)
            SB = [[(ps[0], ("ps", 0)), (ps[1], ("ps", 1)), (ps[2], ("ps", 2))], [(ps[3], ("ps", 3)), (ps[4], ("ps", 4)), (ps[5], ("ps", 5))]]
            OB = [(ps[6], ("ps", 6)), (ps[7], ("ps", 7))]
            units = []

            def emit_S(k):
                U = units[k]
                nq = U["nq"]
                u = U["u"]
                st_ = SB[k % 2]
                b_, kb = st_[2]
                for hl in range(4):
                    hp = 2 * u + hl // 2
                    hh = hl % 2
                    qap, qk = U["qT"](hh, hp)
                    kap, kk = U["part"][2](hh, hp)
                    ebp = U["part"][4]
                    mm(b_[:, hl * 64:hl * 64 + nq], kap, qap, True, ebp is None, kk + qk, [kb])
                    if ebp is not None:
                        ebap, ebk = ebp(2 * hp + hh)
                        mm(b_[:, hl * 64:hl * 64 + nq], jb[:, :], ebap, False, True, ebk + ["jb"], [kb])
                for hpl in range(2):
                    hp = 2 * u + hpl
                    b_, kb = st_[hpl]
                    for hh in range(2):
                        qap, qk = U["qT"](hh, hp)
                        for si in range(4):
                            kap, kk = U["full"][si][0](hh, hp)
                            col = (hh * 4 + si) * 64
                            eb = U["full"][si][2]
                            mm(b_[:, col:col + nq], kap, qap, True, eb is None, kk + qk, [kb])
                            if eb is not None:
                                ebap, ebk = eb(2 * hp + hh)
                                mm(b_[:, col:col + nq], jb[:, :], ebap, False, True, ebk + ["jb"], [kb])

            def emit_E(k):
                U = units[k]
                nq = U["nq"]
                u = U["u"]
                st_ = SB[k % 2]
                ptx = PTX[k % 2]
                kptx = kPTX[k % 2]
                lo, hi, _, _, ebp, ptzl, kptz = U["part"]
                ptz = ptzl[k % 2]
                kptz = kptz[k % 2]
                b_, kb = st_[2]
                act(ptz[lo:hi, :, 0:nq], b_[lo:hi, 0:256].rearrange("p (h q) -> p h q", h=4)[:, :, 0:nq], AF.Exp, [kb], kptz, scale=0.125)
                for hpl in range(2):
                    hp = 2 * u + hpl
                    b_, kb = st_[hpl]
                    act(ptx[:, hpl, :, :, 0:nq], b_[:].rearrange("p (b s q) -> p b s q", b=2, s=4)[:, :, :, 0:nq], AF.Exp, [kb], kptx, scale=0.125)

            def emit_PV(k):
                U = units[k]
                nq = U["nq"]
                u = U["u"]
                ptx = PTX[k % 2]
                kptx = kPTX[k % 2]
                lo, hi, _, vpart, _, ptzl, kptz = U["part"]
                ptz = ptzl[k % 2]
                kptz = kptz[k % 2]
                ob, kob = OB[k % 2]
                for hl in range(4):
                    h = 4 * u + hl
                    for si in range(4):
                        vap, vk = U["full"][si][1](h)
                        mm(ob[0:nq, hl * 65:hl * 65 + 65], ptx[:, hl // 2, hl % 2, si, 0:nq], vap, si == 0, False, kptx + vk, [kob])
                    vap, vk = vpart(h)
                    mm(ob[0:nq, hl * 65:hl * 65 + 65], ptz[:, hl, 0:nq], vap, False, True, kptz + vk, [kob])
                ov = ob[0:nq, 0:260].rearrange("p (h d) -> p h d", h=4)
                rec = REC[k % 2]
                krec = ("REC", k % 2)
                P.add("dve", lambda e_, ov=ov, rec=rec, nq=nq: e_.reciprocal(out=rec[0:nq, 0:4], in_=ov[:, :, 64]), [kob], [krec])
                tt(YD2[k % 2][0:nq, :].rearrange("p (h d) -> p h d", h=4), ov[:, :, 0:64], rec[0:nq, 0:4, None].to_broadcast([nq, 4, 64]),
                   ALU.mult, [kob, krec], kYD2[k % 2])

            def emit_T(k):
                U = units[k]
                nq = U["nq"]
                u = U["u"]
                ob, kob = OB[k % 2]
                for hpl in range(2):
                    mm(ob[:, 384 + hpl * 64:384 + hpl * 64 + nq], YD2[k % 2][:, hpl * 128:(hpl + 1) * 128], identb[:, 0:nq], True, True,
                       kYD2[k % 2] + ["identb"], [kob])
                U["out"](u, ob[:, 384:512].rearrange("p (a q) -> p a q", a=2)[:, :, 0:nq], kob)

            def run_units():
                n = len(units)
                for k in range(n + 2):
                    if k < n:
                        emit_S(k)
                        emit_E(k)
                    if 1 <= k <= n:
                        emit_PV(k - 1)
                    if 2 <= k <= n + 1:
                        emit_T(k - 2)
                del units[:]

            def eb_full(THt, kTH, nq):
                return lambda h: (THt[:, h, 0:nq], [kTH])

            def samp_bufs():
                CK = pg(6, 2)[:, 0:2048].rearrange("p (m n) -> p m n", m=4)
                CV = pg(8, 2)[:, 0:2048].rearrange("p (m n) -> p m n", m=4)
                return CK, kpg(6, 2), CV, kpg(8, 2)

            def sample_prefetch(s_):
                CK, kCK, CV, kCV = samp_bufs()
                dma(CK, ck[o, s_].rearrange("(m p) n -> p m n", p=128), [], kCK, q="pool")
                dma(CV, cv[o, s_].rearrange("(m p) n -> p m n", p=128), [], kCV, q="pool")
            if ps_ == 1 and cfg["odd_sattn"]:
                sample_prefetch(0)
            for c in ((cfg["attn_chunks"] if cfg["attn_chunks"] is not None else range(16)) if cfg["odd_attn"] else []):
                m0 = c // 2

                def kblk(m):
                    return lambda hh, hp, m=m: (KTw[:, hp, m * 128:(m + 1) * 128], kKT)

                def vblk(m):
                    return lambda h, m=m: (Vw[:, m, h, :], kV)
                if c % 2 == 0:
                    full = [(kblk(m0 + i), vblk(m0 + i), eb_full(TH65[o], ("TH65h", o), 64) if i == 3 else None) for i in range(4)]
                    part = (0, 64, kblk(m0 + 4), vblk(m0 + 4), (lambda h: (THB[o][:, h, :], [("THBh", o)])), PTZe, kPTZe)
                else:
                    full = [(kblk(m0 + 1 + i), vblk(m0 + 1 + i), eb_full(TH1[o], ("TH1h", o), 64) if i == 3 else None) for i in range(4)]
                    part = (64, 128, kblk(m0), vblk(m0), None, PTZo, kPTZo)

                def qT_fn(hh, hp, c=c):
                    return QTZ[:, 2 * hp + hh, c * 64:(c + 1) * 64], kQT

                def out_fn(u, view, kob, c=c):
                    act(HY[:, 4 + 2 * u:6 + 2 * u, c * 64:(c + 1) * 64], view, AF.Copy, [kob],
                        [("HY", 4 + 2 * u + a_, (c // 8) * 512) for a_ in range(2)])
                for u in range(2):
                    units.append(dict(nq=64, u=u, qT=qT_fn, full=full, part=part, out=out_fn))
            run_units()
            if ps_ == 0:
                dma(kt_stash[o].rearrange("p (h t) -> p h t", h=4), KTw[:, :, 1024:1536], kKT, [("kt_stash", o)])
                dma(v_stash[o].rearrange("p (m h d) -> p m h d", m=4, h=8), Vw[:, 8:12], kV, [("v_stash", o)])

            if ps_ == 1 and cfg["odd_sattn"]:
                KTs = pg(0, 3)[:, 0:4 * 640].rearrange("p (h t) -> p h t", h=4)
                kKTs = kpg(0, 3)
                Vs = pg(3, 3)[:, 0:5 * 520].rearrange("p (m h d) -> p m h d", m=5, h=8)
                kVs = kpg(3, 3)
                for s in range(2):
                    CK, kCK, CV, kCV = samp_bufs()
                    mset(pg(35), 0.0, kpg(35) + FINE)
                    mset(KTs[:, :, 512:640], 0.0, kKTs)
                    for hp in range(4):
                        b2, kb2 = psget("aux")
                        b2v = b2[:].bitcast(BF16)
                        for m in range(4):
                            tr(b2v[:, m * 128:(m + 1) * 128], CK[:, m, hp * 128:(hp + 1) * 128], identb[:], kCK + ["identb"], [kb2])
                        act(KTs[:, hp, 0:512], b2v[:, 0:512], AF.Copy, [kb2], kKTs)
                        cp(KTs[:, hp, 512:528], KSN[:, hp, 16 * s:16 * s + 16], [("KSN", hp)], kKTs)
                    act(Vs[:, 0:4, :, 0:64], CV.rearrange("p m (h d) -> p m h d", h=8), AF.Copy, kCV, kVs)
                    mset(Vs[:, 0:4, :, 64:65], 1.0, kVs)
                    mset(Vs[:, 4], 0.0, kVs)
                    cp(Vs[0:16, 4, :, 0:64], VSN[:, s, :, 0:64], [("VSN", s)], kVs)
                    mset(Vs[0:16, 4, :, 64:65], 1.0, kVs)
                    if s == 0:
                        sample_prefetch(1)

                    def kblk_s(m):
                        return lambda hh, hp, m=m: (KTs[:, hp, m * 128:(m + 1) * 128], kKTs)

                    def vblk_s(m):
                        return lambda h, m=m: (Vs[:, m, h, :], kVs)
                    full = [(kblk_s(i), vblk_s(i), eb_full(TH65[o], ("TH65h", o), 16) if i == 3 else None) for i in range(4)]
                    part = (0, 16, kblk_s(4), vblk_s(4), (lambda h: (THC[o][:, h, :], [("THCh", o)])), PTZs, kPTZs)

                    def qT_fn(hh, hp, s=s):
                        return QTZ[:, 2 * hp + hh, 1024 + 16 * s:1040 + 16 * s], kQT

                    def out_fn(u, view, kob, s=s):
                        act(HY[:, 4 + 2 * u:6 + 2 * u, 1024 + 16 * s:1040 + 16 * s], view, AF.Copy, [kob],
                            [("HY", 4 + 2 * u + a_, 1024) for a_ in range(2)])
                    for u in range(2):
                        units.append(dict(nq=16, u=u, qT=qT_fn, full=full, part=part, out=out_fn))
                    run_units()
            P.add("dve", lambda e_: e_.memset(REC[0][0:1, 0:1], 0.0), FINE + [("REC", 0)], kpg(12) + kpg(35) + [("REC", 0)])

        for ps_ in range(2):
            tiles = tiles_of(ps_)
            for ch in range(8):
                xs_ = XS[ch % 2]
                kx = KXS[ch % 2]
                dma(xs_, xp[ps_ * NP + ch * 128:ps_ * NP + (ch + 1) * 128, :], [], kx)
                for half in range(2):
                    b_, kb = psget("mm")
                    for c4 in range(4):
                        c = half * 4 + c4
                        tr(b_[:, c4 * 128:(c4 + 1) * 128], xs_[:, c * 128:(c + 1) * 128], ident[:], kx + ["ident"], [kb])
                    act(X[:, half * 4:half * 4 + 4, ch * 128:(ch + 1) * 128], b_[:].rearrange("p (c t) -> p c t", c=4), AF.Copy, [kb],
                        [("X", half * 4 + c4, (ch // 4) * 512) for c4 in range(4)])
            if ps_ == 1:
                dma(XS[0][0:32, :], xs, [], KXS[0])
                b_, kb = psget("mm")
                for c in range(8):
                    tr(b_[:, c * 32:(c + 1) * 32], XS[0][0:32, c * 128:(c + 1) * 128], ident[0:32, 0:32], KXS[0] + ["ident"], [kb])
                act(X[:, :, 1024:1056], b_[:, 0:256].rearrange("p (c t) -> p c t", c=8), AF.Copy, [kb], [("X", c, 1024) for c in range(8)])
            llist = list(cfg["layer_list"] if cfg["layer_list"] is not None else range(NL))
            normed = False
            for li, l in enumerate(llist):
                if cfg["mixer"]:
                    if not normed:
                        pre_norm(0, l, tiles)
                    normed = False
                    nxt = (lambda t0, n, part=None, l=l: pre_norm_tile(2, l, t0, n, part)) if cfg["ffn"] else None
                    if l % 2 == 0:
                        even_mixer(ps_, l, tiles)
                        dense_out_residual(WO_v, kpg(WO_PG, 8), 8, lambda k, t0, n: YE_v[:, k, t0:t0 + n],
                                           lambda k, t0: kpg(27 + k), 1, l, tiles, after_tile=nxt)
                    else:
                        odd_mixer(ps_, l, tiles)
                        dense_out_residual(WO_v, kpg(WO_PG, 8), 8, lambda k, t0, n: HY[:, k, t0:t0 + n],
                                           lambda k, t0: [("HY", k, t0)], 1, l, tiles, after_tile=nxt)
                    normed = nxt is not None
                if cfg["ffn"]:
                    if not normed:
                        pre_norm(2, l, tiles)
                    normed = False
                    wd_th = resident_thunks(wd[l], KFF, WD_PG)
                    def gu_load(j):
                        sg_ = load_stage(wg[l], j * 256)
                        su_ = load_stage(wu[l], j * 256)
                        wd_th.pop(0)()
                        return (sg_, su_)

                    def gu_body(j, stg, t0, n):
                        sg_, su_ = stg
                        for jj in range(2):
                            blk = 2 * j + jj
                            bg_, kbg = fm_block(sg_, jj, t0, n)
                            bu_, kbu = fm_block(su_, jj, t0, n)
                            sgi = blk % 2
                            act(SG[sgi][:, 0:n], bg_[:, 0:n], AF.Silu, [kbg], [("SG", sgi)])
                            tt(A_v[:, blk, t0:t0 + n], bu_[:, 0:n], SG[sgi][:, 0:n], ALU.mult, [kbu, ("SG", sgi)], [("AR", blk)])
                    two_stage_split([lambda j=j: gu_load(j) for j in range(2)], gu_body, tiles)
                    for j in range(2, 11):
                        stg = gu_load(j)
                        for (t0, n) in tiles:
                            gu_body(j, stg, t0, n)
                    nxt = None
                    if li + 1 < len(llist) and cfg["mixer"]:
                        nxt = (lambda t0, n, part=None, l2=llist[li + 1]: pre_norm_tile(0, l2, t0, n, part))
                    dense_out_residual(WD_v, kpg(WD_PG, 22), KFF, lambda k, t0, n: A_v[:, k, t0:t0 + n],
                                       lambda k, t0: [("AR", k)], 3, l, tiles, after_tile=nxt)
                    normed = nxt is not None
            for ch in range(8):
                xs_ = XS[ch % 2]
                kx = KXS[ch % 2]
                for half in range(2):
                    b_, kb = psget("mm")
                    for c4 in range(4):
                        c = half * 4 + c4
                        tr(b_[:, c4 * 128:(c4 + 1) * 128], X[:, c, ch * 128:(ch + 1) * 128], ident[:], [("X", c, (ch // 4) * 512), "ident"], [kb])
                    act(xs_[:, half * 512:(half + 1) * 512], b_[:], AF.Copy, [kb], kx)
                dma(y_p[ps_ * NP + ch * 128:ps_ * NP + (ch + 1) * 128, :], xs_, kx, [("y_p", ps_, ch)])
                OUT_KEYS.append(("y_p", ps_, ch))
            if ps_ == 1:
                for half in range(2):
                    b_, kb = psget("mm")
                    for c4 in range(4):
                        c = half * 4 + c4
                        tr(b_[0:32, c4 * 128:(c4 + 1) * 128], X[:, c, 1024:1056], ident[:], [("X", c, 1024), "ident"], [kb])
                    act(XS[0][0:32, half * 512:(half + 1) * 512], b_[0:32, :], AF.Copy, [kb], KXS[0])
                dma(y_s, XS[0][0:32, :], KXS[0], ["y_s"])
                OUT_KEYS.append("y_s")
        P.add("sp", lambda e: None, reads=list(OUT_KEYS))
        with nc.allow_non_contiguous_dma(reason="small constant / broadcast loads"):
            P.emit(st)
        nc._prog_stats = (len(P.ops), dict(P.sig_counts))
    return nc


_CONSTS = None


def _consts():
    global _CONSTS
    if _CONSTS is None:
        ident = np.eye(128, dtype=np.float32)
        jmat = np.ascontiguousarray(ident[::-1])
        tri = np.triu(np.ones((128, 128), dtype=np.float32))
        rc = np.zeros((128, 4, 16), dtype=np.float32)
        for g, win in enumerate((2, 4, 8, 16)):
            rc[:, g, :] = 1.0 / np.minimum(np.arange(16) + 1, win)
        _CONSTS = dict(c_ident=ident, c_jmat=jmat, c_tri=tri, c_rcnt=rc.reshape(128, 64))
    return _CONSTS


def make_in_maps(inp):
    f = lambda a: np.ascontiguousarray(np.asarray(a, dtype=np.float32))
    norms = f(np.stack([inp["norm_mix_pre"], inp["norm_mix_post"], inp["norm_ffn_pre"], inp["norm_ffn_post"]]))
    shared = dict(
        norms=norms, wg=f(inp["ffn_w_gate"]), wu=f(inp["ffn_w_up"]), wd=f(inp["ffn_w_down"]),
        evin=f(inp["ev_w_in"]), evout=f(inp["ev_w_out"]), avg=f(inp["a_v_gain"]), asw=f(inp["a_spatial_w"]),
        asb=f(np.asarray(inp["a_spatial_b"]).reshape(2, 512)), bcw=f(inp["b_conv_w"]), odin=f(inp["od_w_in"]), odout=f(inp["od_w_out"]),
        cgw=f(inp["c_group_w"]), csc=f(inp["c_scale"]), drb=f(inp["d_rel_bias"]), **_consts())
    xp = f(inp["x_prompt"])
    xs = f(inp["x_sample"])
    cc = f(inp["cache_conv"])
    cpo = f(inp["cache_pool"])
    ck = f(inp["cache_k"]).reshape(2, 16, 512, 512)
    cv = f(inp["cache_v"]).reshape(2, 16, 512, 512)
    maps = []
    for i in range(NCORES):
        m = dict(shared)
        m["xp"] = xp[i]
        m["xs"] = np.ascontiguousarray(xs[2 * i:2 * i + 2].reshape(32, D))
        m["cconv"] = np.ascontiguousarray(cc[:, 2 * i:2 * i + 2])
        m["cpool"] = np.ascontiguousarray(cpo[:, 2 * i:2 * i + 2])
        m["ck"] = np.ascontiguousarray(ck[:, 2 * i:2 * i + 2])
        m["cv"] = np.ascontiguousarray(cv[:, 2 * i:2 * i + 2])
        maps.append(m)
    return maps


def assemble(res):
    R = res
    y_p = np.stack([r["y_p"] for r in R])
    y_s = np.concatenate([r["y_s"].reshape(2, 16, D) for r in R], axis=0)
    conv_p = np.stack([r["o_conv_p"] for r in R], axis=1)
    pool_p = np.stack([r["o_pool_p"] for r in R], axis=1)
    k_p = np.stack([r["o_k_p"].reshape(2, 512, 8, 64) for r in R], axis=1)
    v_p = np.stack([r["o_v_p"].reshape(2, 512, 8, 64) for r in R], axis=1)
    av_s = np.concatenate([r["o_av_s"].reshape(2, 2, 16, 512) for r in R], axis=1)
    conv_s = np.concatenate([r["o_conv_s"] for r in R], axis=1)
    pool_s = np.concatenate([r["o_pool_s"] for r in R], axis=1)
    k_s = np.concatenate([r["o_k_s"].reshape(2, 2, 16, 8, 64) for r in R], axis=1)
    v_s = np.concatenate([r["o_v_s"].reshape(2, 2, 16, 8, 64) for r in R], axis=1)
    outs = (y_p, y_s, conv_p, pool_p, k_p, v_p, av_s, conv_s, pool_s, k_s, v_s)
    return tuple(np.ascontiguousarray(o, dtype=np.float32) for o in outs)


_NC_CACHE = {}


def kernel(**inputs):
    key = "full"
    if key not in _NC_CACHE:
        _NC_CACHE[key] = build_program()
    nc = _NC_CACHE[key]
    in_maps = make_in_maps(inputs)
    res = run_bass_kernel_spmd(nc, in_maps, core_ids=list(range(NCORES)))
    return assemble(res.results)
```
